# Optimizing a Trainium2 kernel written in Bass

```python
import jax, jax.numpy as jnp
from jax import lax
import numpy as np

D_MODEL = 1024
BATCH = 8
SEQ = 4096
DEPTH = 4

GRID_W = 64
CTX_LEN = 256
D_MIX = D_MODEL
GROUP_W = D_MIX // 4
D_FF = 4 * D_MODEL
N_MOD = 6
EPS = 1e-6
ROPE_BASE = 10000.0
Q_BLOCK = 128
CONV_WIDTH = 31
MLA_HEADS = 4
MLA_NOPE = 64
MLA_ROPE = 32
MLA_V = 64
MLA_Q_RANK = 256
MLA_KV_RANK = 128
GQA_HEADS = 4
GQA_KV_HEADS = 2
GQA_HEAD_DIM = 64
RET_HEADS = 4
RET_QK = 32
RET_V = 64
RET_CHUNK = 128

IN_SIZES = (2 * GROUP_W,
            MLA_Q_RANK, MLA_KV_RANK, MLA_ROPE,
            GQA_HEADS * GQA_HEAD_DIM, GQA_KV_HEADS * GQA_HEAD_DIM, GQA_KV_HEADS * GQA_HEAD_DIM,
            RET_HEADS * RET_QK, RET_HEADS * RET_QK, RET_HEADS * RET_V,
            GROUP_W, GROUP_W)
D_IN = sum(IN_SIZES)

kernel_name = 'hybrid_parallel_groups_flow_block'


def rms_norm(x, gain=None):
    xf = x.astype(jnp.float32)
    y = xf * lax.rsqrt(jnp.mean(xf * xf, axis=-1, keepdims=True) + EPS)
    if gain is not None:
        y = y * gain.astype(jnp.float32)
    return y.astype(x.dtype)


def layer_norm(x, gain, bias):
    xf = x.astype(jnp.float32)
    mu = jnp.mean(xf, axis=-1, keepdims=True)
    var = jnp.mean(jnp.square(xf - mu), axis=-1, keepdims=True)
    y = (xf - mu) * lax.rsqrt(var + EPS) * gain.astype(jnp.float32) + bias.astype(jnp.float32)
    return y.astype(x.dtype)


def split_cols(u):
    parts, off = [], 0
    for s in IN_SIZES:
        parts.append(u[..., off:off + s])
        off += s
    return tuple(parts)


def heads(t, n, d):
    b, l, _ = t.shape
    return t.reshape(b, l, n, d).transpose(0, 2, 1, 3)


def merge_heads(o):
    b, hk, g, l, d = o.shape
    return o.reshape(b, hk * g, l, d).transpose(0, 2, 1, 3).reshape(b, l, hk * g * d)


def flip_seq(t):
    return jnp.flip(t, axis=2)


def rope_1d(x, pos):
    half = x.shape[-1] // 2
    freqs = ROPE_BASE ** (-jnp.arange(half, dtype=jnp.float32) / half)
    ang = pos.astype(jnp.float32)[:, None] * freqs[None, :]
    cos, sin = jnp.cos(ang), jnp.sin(ang)
    xf = x.astype(jnp.float32)
    x1, x2 = xf[..., :half], xf[..., half:]
    return jnp.concatenate([x1 * cos - x2 * sin, x1 * sin + x2 * cos], axis=-1).astype(x.dtype)


def rope_2d(x, row, col):
    half = x.shape[-1] // 2
    return jnp.concatenate([rope_1d(x[..., :half], row), rope_1d(x[..., half:], col)], axis=-1)


def grouped_attention(q, k, v):
    b, hk, g, lq, dq = q.shape
    nb = lq // Q_BLOCK
    scale = dq ** -0.5
    qb = jnp.moveaxis(q.reshape(b, hk, g, nb, Q_BLOCK, dq), 3, 0)

    def one_block(qblk):
        s = jnp.einsum('bkgqd,bksd->bkgqs', qblk, k, preferred_element_type=jnp.float32) * scale
        p = jax.nn.softmax(s, axis=-1)
        return jnp.einsum('bkgqs,bksd->bkgqd', p.astype(v.dtype), v)

    o = lax.map(one_block, qb)
    return jnp.moveaxis(o, 0, 3).reshape(b, hk, g, lq, v.shape[-1])


def conformer_conv(u, w_dw, b_dw, ln_g, ln_b, w_pw):
    y = u[..., :GROUP_W] * jax.nn.sigmoid(u[..., GROUP_W:])
    pad = CONV_WIDTH // 2
    y = lax.conv_general_dilated(y, w_dw[:, None, :], window_strides=(1,), padding=((pad, pad),),
                                 dimension_numbers=('NWC', 'WIO', 'NWC'),
                                 feature_group_count=GROUP_W) + b_dw
    y = jax.nn.silu(layer_norm(y, ln_g, ln_b))
    return y @ w_pw


def mla_q(u, g, w_uq, pos):
    q = heads(rms_norm(u, g) @ w_uq, MLA_HEADS, MLA_NOPE + MLA_ROPE)
    if pos is not None:
        q = jnp.concatenate([q[..., :MLA_NOPE], rope_2d(q[..., MLA_NOPE:], *pos)], axis=-1)
    return q[:, :, None]


def mla_kv(u_kv, u_kpe, g, w_ukv, pos):
    kv = heads(rms_norm(u_kv, g) @ w_ukv, MLA_HEADS, MLA_NOPE + MLA_V)
    k_pe = u_kpe
    if pos is not None:
        k_pe = rope_2d(k_pe, *pos)
    b, h, l, _ = kv.shape
    k = jnp.concatenate([kv[..., :MLA_NOPE], jnp.broadcast_to(k_pe[:, None], (b, h, l, MLA_ROPE))], axis=-1)
    return k, kv[..., MLA_NOPE:]


def gqa_q(u, g, pos):
    q = rms_norm(heads(u, GQA_HEADS, GQA_HEAD_DIM), g)
    if pos is not None:
        q = rope_2d(q, *pos)
    b, h, l, d = q.shape
    return q.reshape(b, GQA_KV_HEADS, h // GQA_KV_HEADS, l, d)


def gqa_kv(uk, uv, g, pos):
    k = rms_norm(heads(uk, GQA_KV_HEADS, GQA_HEAD_DIM), g)
    if pos is not None:
        k = rope_2d(k, *pos)
    return k, heads(uv, GQA_KV_HEADS, GQA_HEAD_DIM)


def ret_qkv(uq, uk, uv):
    return (heads(uq, RET_HEADS, RET_QK), heads(uk, RET_HEADS, RET_QK) * (RET_QK ** -0.5),
            heads(uv, RET_HEADS, RET_V))


def retention_chunked(q, k, v, log_gamma, s0, with_output):
    b, h, l, dk = q.shape
    dv = v.shape[-1]
    n = l // RET_CHUNK
    qc = q.astype(jnp.float32).reshape(b, h, n, RET_CHUNK, dk)
    kc = k.astype(jnp.float32).reshape(b, h, n, RET_CHUNK, dk)
    vc = v.astype(jnp.float32).reshape(b, h, n, RET_CHUNK, dv)
    idx = jnp.arange(RET_CHUNK, dtype=jnp.float32)
    lg = log_gamma[:, None]
    k_decay = jnp.exp(lg * (RET_CHUNK - 1.0 - idx))
    chunk_kv = jnp.einsum('bhnjd,bhnje->bhnde', kc * k_decay[:, None, :, None], vc)
    chunk_decay = jnp.exp(log_gamma * RET_CHUNK)[:, None, None]

    def step(s, kv_j):
        return s * chunk_decay + kv_j, s

    s_final, s_prev = lax.scan(step, s0, jnp.moveaxis(chunk_kv, 2, 0))
    if not with_output:
        return None, s_final
    s_prev = jnp.moveaxis(s_prev, 0, 2)
    diff = idx[:, None] - idx[None, :]
    decay = jnp.where(diff >= 0, jnp.exp(lg[:, :, None] * jnp.maximum(diff, 0.0)), 0.0)
    scores = jnp.einsum('bhnid,bhnjd->bhnij', qc, kc) * decay[:, None]
    inner = jnp.einsum('bhnij,bhnje->bhnie', scores, vc)
    q_decay = jnp.exp(lg * (idx + 1.0))
    cross = jnp.einsum('bhnid,bhnde->bhnie', qc * q_decay[:, None, :, None], s_prev)
    return (inner + cross).reshape(b, h, l, dv), s_final


def ret_readout(o, gain, gate):
    b, h, l, d = o.shape
    y = rms_norm(o, gain[:, None, :]).transpose(0, 2, 1, 3).reshape(b, l, h * d)
    return y.astype(gate.dtype) * jax.nn.silu(gate)


def adaln(cond, w, b):
    m = (jax.nn.silu(cond) @ w + b).reshape(cond.shape[0], N_MOD, D_MODEL)
    return tuple(m[:, i, None, :] for i in range(N_MOD))


def modulate(x, shift, scale):
    return rms_norm(x) * (1.0 + scale) + shift


def sq_relu_mlp(h, w1, w2):
    return jnp.square(jax.nn.relu(h @ w1)) @ w2


def mixing_sublayer(h_ctx, h_lat, pos, ctx_out, w_in, w_out, conv_dw, conv_b, conv_ln_g, conv_ln_b,
                    conv_pw, mla_q_g, mla_kv_g, mla_uq, mla_ukv, gqa_q_g, gqa_k_g, ret_decay, ret_norm_g):
    (a_c, cq_c, ckv_c, kpe_c, q_c, k_c, v_c, rq_c, rk_c, rv_c, gf_c, gb_c) = split_cols(h_ctx @ w_in)
    (a_l, cq_l, ckv_l, kpe_l, q_l, k_l, v_l, rq_l, rk_l, rv_l, gf_l, gb_l) = split_cols(h_lat @ w_in)
    conv_args = (conv_dw, conv_b, conv_ln_g, conv_ln_b, conv_pw)
    ya_l = conformer_conv(a_l, *conv_args)
    mk_c, mv_c = mla_kv(ckv_c, kpe_c, mla_kv_g, mla_ukv, None)
    mk_l, mv_l = mla_kv(ckv_l, kpe_l, mla_kv_g, mla_ukv, pos)
    yb_l = merge_heads(grouped_attention(mla_q(cq_l, mla_q_g, mla_uq, pos),
                                         jnp.concatenate([mk_c, mk_l], axis=2),
                                         jnp.concatenate([mv_c, mv_l], axis=2)))
    gk_c, gv_c = gqa_kv(k_c, v_c, gqa_k_g, None)
    gk_l, gv_l = gqa_kv(k_l, v_l, gqa_k_g, pos)
    yc_l = merge_heads(grouped_attention(gqa_q(q_l, gqa_q_g, pos),
                                         jnp.concatenate([gk_c, gk_l], axis=2),
                                         jnp.concatenate([gv_c, gv_l], axis=2)))
    log_g = jax.nn.log_sigmoid(ret_decay.astype(jnp.float32))
    tq_c, tk_c, tv_c = ret_qkv(rq_c, rk_c, rv_c)
    tq_l, tk_l, tv_l = ret_qkv(rq_l, rk_l, rv_l)
    s0 = jnp.zeros((h_ctx.shape[0], RET_HEADS, RET_QK, RET_V), jnp.float32)
    oc_f, s_f = retention_chunked(tq_c, tk_c, tv_c, log_g[0], s0, ctx_out)
    oc_b, s_b = retention_chunked(flip_seq(tq_c), flip_seq(tk_c), flip_seq(tv_c), log_g[1], s0, ctx_out)
    ol_f, _ = retention_chunked(tq_l, tk_l, tv_l, log_g[0], s_f, True)
    ol_b, _ = retention_chunked(flip_seq(tq_l), flip_seq(tk_l), flip_seq(tv_l), log_g[1], s_b, True)
    yd_l = ret_readout(ol_f, ret_norm_g[0], gf_l) + ret_readout(flip_seq(ol_b), ret_norm_g[1], gb_l)
    y_lat = jnp.concatenate([ya_l, yb_l, yc_l, yd_l], axis=-1) @ w_out
    if not ctx_out:
        return None, y_lat
    ya_c = conformer_conv(a_c, *conv_args)
    yb_c = merge_heads(grouped_attention(mla_q(cq_c, mla_q_g, mla_uq, None), mk_c, mv_c))
    yc_c = merge_heads(grouped_attention(gqa_q(q_c, gqa_q_g, None), gk_c, gv_c))
    yd_c = ret_readout(oc_f, ret_norm_g[0], gf_c) + ret_readout(flip_seq(oc_b), ret_norm_g[1], gb_c)
    y_ctx = jnp.concatenate([ya_c, yb_c, yc_c, yd_c], axis=-1) @ w_out
    return y_ctx, y_lat


def setup_inputs(seed: int = 0) -> dict:
    key = jax.random.key(seed)
    ks = jax.random.split(key, 24)

    def nrm(k, shape, scale):
        return jax.random.normal(k, shape, jnp.float32) * scale

    a = 5.0 + np.arange(RET_HEADS, dtype=np.float32)
    decay_logit = jnp.asarray(np.log(2.0 ** a - 1.0).astype(np.float32))
    return {
        'x': nrm(ks[0], (BATCH, SEQ, D_MODEL), 1.0),
        'c': nrm(ks[1], (BATCH, D_MODEL), 1.0),
        'ctx': nrm(ks[2], (BATCH, CTX_LEN, D_MODEL), 1.0),
        'c_ctx': nrm(ks[3], (D_MODEL,), 1.0),
        'w_mod': nrm(ks[4], (DEPTH, D_MODEL, N_MOD * D_MODEL), 0.5 * D_MODEL ** -0.5),
        'b_mod': nrm(ks[5], (DEPTH, N_MOD * D_MODEL), 0.02),
        'w_in': nrm(ks[6], (DEPTH, D_MODEL, D_IN), D_MODEL ** -0.5),
        'w_out': nrm(ks[7], (DEPTH, D_MIX, D_MODEL), D_MIX ** -0.5),
        'conv_dw': nrm(ks[8], (DEPTH, CONV_WIDTH, GROUP_W), CONV_WIDTH ** -0.5),
        'conv_b': nrm(ks[9], (DEPTH, GROUP_W), 0.02),
        'conv_ln_g': 1.0 + nrm(ks[10], (DEPTH, GROUP_W), 0.02),
        'conv_ln_b': nrm(ks[11], (DEPTH, GROUP_W), 0.02),
        'conv_pw': nrm(ks[12], (DEPTH, GROUP_W, GROUP_W), GROUP_W ** -0.5),
        'mla_q_g': 1.0 + nrm(ks[13], (DEPTH, MLA_Q_RANK), 0.02),
        'mla_kv_g': 1.0 + nrm(ks[14], (DEPTH, MLA_KV_RANK), 0.02),
        'mla_uq': nrm(ks[15], (DEPTH, MLA_Q_RANK, MLA_HEADS * (MLA_NOPE + MLA_ROPE)), MLA_Q_RANK ** -0.5),
        'mla_ukv': nrm(ks[16], (DEPTH, MLA_KV_RANK, MLA_HEADS * (MLA_NOPE + MLA_V)), MLA_KV_RANK ** -0.5),
        'gqa_q_g': 1.0 + nrm(ks[17], (DEPTH, GQA_HEAD_DIM), 0.02),
        'gqa_k_g': 1.0 + nrm(ks[18], (DEPTH, GQA_HEAD_DIM), 0.02),
        'ret_decay': decay_logit[None, None, :] + nrm(ks[19], (DEPTH, 2, RET_HEADS), 0.1),
        'ret_norm_g': 1.0 + nrm(ks[20], (DEPTH, 2, RET_HEADS, RET_V), 0.02),
        'mlp_w1': nrm(ks[21], (DEPTH, D_MODEL, D_FF), D_MODEL ** -0.5),
        'mlp_w2': nrm(ks[22], (DEPTH, D_FF, D_MODEL), D_FF ** -0.5),
        'final_g': 1.0 + nrm(ks[23], (D_MODEL,), 0.02),
    }


def reference(x, c, ctx, c_ctx, w_mod, b_mod, w_in, w_out, conv_dw, conv_b, conv_ln_g, conv_ln_b,
              conv_pw, mla_q_g, mla_kv_g, mla_uq, mla_ukv, gqa_q_g, gqa_k_g, ret_decay, ret_norm_g,
              mlp_w1, mlp_w2, final_g):
    n_lat = x.shape[1]
    n_rows = n_lat // GRID_W
    row = jnp.repeat(jnp.arange(n_rows, dtype=jnp.int32), GRID_W)
    col = jnp.tile(jnp.arange(GRID_W, dtype=jnp.int32), n_rows)
    pos = (row, col)
    cx = ctx
    for layer in range(DEPTH):
        last = layer == DEPTH - 1
        sh1, sc1, g1, sh2, sc2, g2 = adaln(c, w_mod[layer], b_mod[layer])
        csh1, csc1, cg1, csh2, csc2, cg2 = adaln(c_ctx[None, :], w_mod[layer], b_mod[layer])
        y_ctx, y_lat = mixing_sublayer(
            modulate(cx, csh1, csc1), modulate(x, sh1, sc1), pos, not last,
            w_in[layer], w_out[layer], conv_dw[layer], conv_b[layer], conv_ln_g[layer], conv_ln_b[layer],
            conv_pw[layer], mla_q_g[layer], mla_kv_g[layer], mla_uq[layer], mla_ukv[layer],
            gqa_q_g[layer], gqa_k_g[layer], ret_decay[layer], ret_norm_g[layer])
        x = x + g1 * y_lat
        x = x + g2 * sq_relu_mlp(modulate(x, sh2, sc2), mlp_w1[layer], mlp_w2[layer])
        if not last:
            cx = cx + cg1 * y_ctx
            cx = cx + cg2 * sq_relu_mlp(modulate(cx, csh2, csc2), mlp_w1[layer], mlp_w2[layer])
    return rms_norm(x, final_g)
```

```python
import numpy as np
import concourse.bass as bass
import concourse.mybir as mybir
from concourse.bass_utils import run_bass_kernel_spmd
from contextlib import ExitStack

F32 = mybir.dt.float32
BF16 = mybir.dt.bfloat16
AF = mybir.ActivationFunctionType
ALU = mybir.AluOpType

D = 1024
L = 4096
CT = 256
T = L + CT
DEPTH = 4
NU = 2880
EPS = 1e-6
TILES = [(0, 256)] + [(256 + 512 * i, 512) for i in range(8)]
NKT = T // 128
NV = 29


ATTACH_WAITS = True
ATTACH_DMA = True


class Buf:
    __slots__ = ("w", "r", "sem", "cnt")

    def __init__(self):
        self.w = []
        self.r = []
        self.sem = None
        self.cnt = 0


class FW:
    def __init__(self, nc):
        self.nc = nc
        self.es = ExitStack()
        self.eng = {}
        for name, h in (("pe", nc.tensor), ("act", nc.scalar), ("dve", nc.vector), ("pool", nc.gpsimd), ("sp", nc.sync)):
            sem = self.es.enter_context(nc.semaphore(name + "_ms"))
            self.eng[name] = dict(h=h, sem=sem, cnt=0, seen={}, name=name)
        self.free_sems = []
        self.phase_sems = []
        self.nsem = 0
        self.outstanding = []
        self.uid = 0
        self.pool_sems = [dict(sem=self.es.enter_context(nc.semaphore("psem%d" % i)), cnt=0) for i in range(8)]
        self.pool_rr = 0

    def get_sem(self):
        if self.free_sems:
            s = self.free_sems.pop()
        else:
            self.nsem += 1
            s = self.es.enter_context(self.nc.semaphore("dsem%d" % self.nsem))
        self.phase_sems.append(s)
        return s

    def _wait(self, e, ev):
        if ev is None:
            return
        sem, val = ev
        key = id(sem)
        if e["seen"].get(key, 0) >= val:
            return
        if sem is e["sem"] and e["name"] == "pe":
            return
        e["h"].wait_ge(sem, val)
        e["seen"][key] = val

    def _needed(self, e, reads, writes):
        need = {}
        for b in reads:
            for (sem, val) in b.w:
                if need.get(id(sem), (None, 0))[1] < val:
                    need[id(sem)] = (sem, val)
        for b in writes:
            for (sem, val) in list(b.w) + list(b.r):
                if need.get(id(sem), (None, 0))[1] < val:
                    need[id(sem)] = (sem, val)
        out = []
        for (sem, val) in need.values():
            if e["seen"].get(id(sem), 0) >= val:
                continue
            if sem is e["sem"] and e["name"] == "pe":
                continue
            out.append((sem, val))
        return out

    def _deps(self, e, reads, writes):
        for ev in self._needed(e, reads, writes):
            self._wait(e, ev)

    def op(self, engname, fn, reads=(), writes=(), inc=True):
        e = self.eng[engname]
        need = self._needed(e, reads, writes)
        attach = None
        if ATTACH_WAITS and need:
            attach = need.pop()
        for ev in need:
            self._wait(e, ev)
        inst = fn(e["h"])
        if attach is not None:
            inst._wait_ge(attach[0], attach[1])
            e["seen"][id(attach[0])] = attach[1]
        if inc:
            e["cnt"] += 1
            inst.then_inc(e["sem"], 1)
            ev = (e["sem"], e["cnt"])
        else:
            ev = (e["sem"], e["cnt"] + 1)
        for b in reads:
            b.r.append(ev)
        for b in writes:
            b.w = [ev]
            b.r = []
        return ev

    def dma(self, qname, out, in_, reads=(), writes=(), sbuf=None, nodeps=False, **kw):
        e = self.eng[qname]
        need = [] if nodeps else self._needed(e, reads, writes)
        if qname == "pool":
            ps = self.pool_sems[self.pool_rr % len(self.pool_sems)]
            self.pool_rr += 1
            if ps["cnt"] > 0 and e["seen"].get(id(ps["sem"]), 0) < ps["cnt"]:
                need = [x for x in need if x[0] is not ps["sem"]] + [(ps["sem"], ps["cnt"])]
            for ev in need:
                self._wait(e, ev)
            ps["cnt"] += 16
            e["h"].dma_start(out=out, in_=in_, **kw).then_inc(ps["sem"], 16)
            ev = (ps["sem"], ps["cnt"])
        else:
            attach = None
            if ATTACH_DMA and need:
                attach = need.pop()
            for ev in need:
                self._wait(e, ev)
            if sbuf.sem is None:
                sbuf.sem = self.get_sem()
            sbuf.cnt += 16
            inst = e["h"].dma_start(out=out, in_=in_, **kw)
            if attach is not None:
                inst._wait_ge(attach[0], attach[1])
                e["seen"][id(attach[0])] = attach[1]
            inst.then_inc(sbuf.sem, 16)
            ev = (sbuf.sem, sbuf.cnt)
        for b in reads:
            b.r.append(ev)
        for b in writes:
            if nodeps:
                b.w = [x for x in b.w if x[0] is not ev[0]] + [ev]
            else:
                b.w = [ev]
                b.r = []
        self.outstanding.append(ev)
        return ev

    def phase_end(self):
        e = self.eng["sp"]
        mx = {}
        for (sem, val) in self.outstanding:
            if mx.get(id(sem), (None, 0))[1] < val:
                mx[id(sem)] = (sem, val)
        for ev in mx.values():
            self._wait(e, ev)
        self.outstanding = []
        self.nc.all_engine_barrier()
        for s in self.phase_sems:
            self.nc.sync.sem_clear(s)
            for e2 in self.eng.values():
                e2["seen"].pop(id(s), None)
            self.free_sems.append(s)
        self.phase_sems = []
        for b in getattr(self, "persistent", []):
            b.w = []
            b.r = []
        for e2 in self.eng.values():
            self.nc.sync.sem_clear(e2["sem"])
            e2["cnt"] = 0
            e2["seen"] = {}
        self.nc.all_engine_barrier()


class Phase:
    def __init__(self, fw):
        self.fw = fw
        self.nc = fw.nc
        self.es = ExitStack()

    def sb(self, shape, dt):
        self.fw.uid += 1
        return self.es.enter_context(self.nc.sbuf_tensor("sb%d" % self.fw.uid, list(shape), dt))

    def ps(self, shape=(128, 512), dt=F32):
        self.fw.uid += 1
        return self.es.enter_context(self.nc.psum_tensor("ps%d" % self.fw.uid, list(shape), dt))

    def ring(self, n, shape, dt, psum=False):
        return Ring([(self.ps(shape, dt) if psum else self.sb(shape, dt), Buf()) for _ in range(n)])

    def close(self):
        self.fw.phase_end()
        self.es.close()


class Ring:
    def __init__(self, items):
        self.items = items
        self.i = -1

    def next(self):
        self.i = (self.i + 1) % len(self.items)
        return self.items[self.i]


class G:
    pass


def cast_load(fw, dst, src, buf, ncols):
    if ncols <= 1024:
        fw.dma("pool", dst, src, writes=[buf], sbuf=buf, nodeps=True)
        return
    a = -(-ncols // 1024)
    while ncols % a:
        a += 1
    fw.dma("pool", dst.rearrange("p (a b) -> p a b", a=a), src.rearrange("p (a b) -> p a b", a=a),
           writes=[buf], sbuf=buf, nodeps=True)


def cast_load3(fw, dst, src, buf):
    fw.dma("pool", dst, src.rearrange("(kc p) n -> p kc n", p=128), writes=[buf], sbuf=buf, nodeps=True)


def emit_mod(fw, g, ph, l):
    wm = ph.ring(2, [128, 8, 1536], BF16)
    mp = ph.ring(2, [2, 512], F32, psum=True)
    tp = ph.ps([128, 48, 2], F32)
    btp = Buf()
    mrow = ph.sb([2, 6 * D], F32)
    bmrow = [Buf() for _ in range(12)]
    for cg in range(4):
        wt, bw = wm.next()
        fw.dma("pool", wt[:, :, :], g.w_mod[l, :, cg * 1536:(cg + 1) * 1536].rearrange("(kc p) n -> p kc n", p=128),
               writes=[bw], sbuf=bw)
        for q in range(3):
            ng = cg * 3 + q
            mpt, bmp = mp.next()
            for kc in range(8):
                fw.op("pe", lambda h: h.matmul(mpt[:, :], g.sl_bf[:, kc, :], wt[:, kc, q * 512:(q + 1) * 512],
                                               start=(kc == 0), stop=(kc == 7)),
                      reads=[bw, g.bconst], writes=[bmp], inc=(kc == 7))
            fw.op("dve", lambda h: h.tensor_copy(out=mrow[:, ng * 512:(ng + 1) * 512], in_=mpt[:, :]), reads=[bmp], writes=[bmrow[ng]])
    for j in range(48):
        fw.op("pe", lambda h: h.matmul(tp[:, j, :], mrow[:, j * 128:(j + 1) * 128], g.ident_f[0:2, 0:2], start=True, stop=True),
              reads=[bmrow[j // 4], g.bconst], writes=[btp], inc=(j == 47))
    for cond in range(2):
        fw.op("dve", lambda h: h.tensor_tensor(out=g.modT[:, l, cond, :], in0=tp[:, :, cond], in1=g.bm[:, l, :],
                                               op=ALU.add), reads=[btp, g.bconst], writes=[g.bmod])
    for j0 in (8, 32):
        fw.op("dve", lambda h: h.tensor_scalar(out=g.modT[:, l, :, j0:j0 + 8], in0=g.modT[:, l, :, j0:j0 + 8],
                                               scalar1=1.0, scalar2=None, op0=ALU.add),
              reads=[g.bmod], writes=[g.bmod])


def rstd_from_sq(fw, g, sq_chunks, bsq, lhs_ones, ssp, bssp, sd, bsd, rstd, brstd, W, nfeat, P=128):
    n = len(sq_chunks)
    for i, ap in enumerate(sq_chunks):
        fw.op("pe", lambda h: h.matmul(ssp[:P, :W], lhs_ones, ap, start=(i == 0), stop=(i == n - 1)),
              reads=[bsq, g.bconst], writes=[bssp], inc=(i == n - 1))
    fw.op("act", lambda h: h.activation(out=sd[:P, :W], in_=ssp[:P, :W], func=AF.Ln, bias=g.epsb[:P, :], scale=1.0 / nfeat),
          reads=[bssp, g.bconst], writes=[bsd])
    fw.op("act", lambda h: h.activation(out=rstd[:P, :W], in_=sd[:P, :W], func=AF.Exp, scale=-0.5), reads=[bsd], writes=[brstd])


def norm_modulate(fw, g, ph_bufs, xt, bxt, hh, bh, W, sc_ap, sh_ap):
    sq, bsq, ssp, bssp, sd, bsd, rstd, brstd, tt, btt = ph_bufs
    fw.op("act", lambda h: h.activation(out=sq[:, :, :W], in_=xt[:, :, :W], func=AF.Square), reads=[bxt], writes=[bsq])
    rstd_from_sq(fw, g, [sq[:, kc, :W] for kc in range(8)], bsq, g.ones_bf[:, :], ssp, bssp, sd, bsd, rstd, brstd, W, D)
    for kc in range(8):
        fw.op("dve", lambda h: h.tensor_tensor(out=tt[:, kc, :W], in0=xt[:, kc, :W], in1=rstd[:, :W], op=ALU.mult),
              reads=[bxt, brstd], writes=[btt[kc]])
        fw.op("act", lambda h: h.activation(out=hh[:, kc, :W], in_=tt[:, kc, :W], func=AF.Identity,
                                            bias=sh_ap(kc), scale=sc_ap(kc)),
              reads=[btt[kc], g.bmod], writes=[bh[kc]])


def norm_modulate_gen(fw, g, ph_bufs, xt, bxt, hh, bh, W, sc_ap, sh_ap):
    sq, bsq, ssp, bssp, sd, bsd, rstd, brstd, tt, btt = ph_bufs
    fw.op("act", lambda h: h.activation(out=sq[:, :, :W], in_=xt[:, :, :W], func=AF.Square), reads=[bxt], writes=[bsq])
    yield
    rstd_from_sq(fw, g, [sq[:, kc, :W] for kc in range(8)], bsq, g.ones_bf[:, :], ssp, bssp, sd, bsd, rstd, brstd, W, D)
    yield
    for kc in range(8):
        fw.op("dve", lambda h: h.tensor_tensor(out=tt[:, kc, :W], in0=xt[:, kc, :W], in1=rstd[:, :W], op=ALU.mult),
              reads=[bxt, brstd], writes=[btt[kc]])
        fw.op("act", lambda h: h.activation(out=hh[:, kc, :W], in_=tt[:, kc, :W], func=AF.Identity,
                                            bias=sh_ap(kc), scale=sc_ap(kc)),
              reads=[btt[kc], g.bmod], writes=[bh[kc]])
        yield


def norm_bufs(ph):
    sq = ph.sb([128, 8, 512], BF16)
    ssp = ph.ps()
    sd = ph.sb([128, 512], F32)
    rstd = ph.sb([128, 512], F32)
    tt = ph.sb([128, 8, 512], F32)
    return (sq, Buf(), ssp, Buf(), sd, Buf(), rstd, Buf(), tt, [Buf() for _ in range(8)])


def phase_inproj(fw, g, l, xsrc):
    ph = Phase(fw)
    win = ph.sb([128, 8, NU], BF16)
    bwin = [Buf() for _ in range(3)]
    for cb in range(3):
        cast_load3(fw, win[:, :, cb * 960:(cb + 1) * 960], g.w_in[l, :, cb * 960:(cb + 1) * 960], bwin[cb])
    xt = ph.ring(2, [128, 8, 512], F32)
    nb = norm_bufs(ph)
    hh2 = [ph.sb([128, 8, 512], BF16) for _ in range(2)]
    bh2 = [[Buf() for _ in range(8)] for _ in range(2)]
    up = ph.ring(4, [128, 512], F32, psum=True)
    ut = [ph.sb([128, 23, 512], BF16) for _ in range(2)]
    but = [[Buf() for _ in range(23)] for _ in range(2)]
    loaded = {}
    NT = len(TILES)

    def load(ti):
        t0, W = TILES[ti]
        x_t, bx = xt.next()
        fw.dma("sp", x_t[:, :, :W], xsrc[:, t0:t0 + W].rearrange("(kc p) t -> p kc t", p=128), writes=[bx], sbuf=bx)
        loaded[ti] = (x_t, bx)

    def norm(ti):
        t0, W = TILES[ti]
        x_t, bx = loaded.pop(ti)
        cond = 1 if ti == 0 else 0
        return norm_modulate_gen(fw, g, nb, x_t, bx, hh2[ti % 2], bh2[ti % 2], W,
                                 lambda kc: g.modT[:, l, cond, 8 + kc:9 + kc], lambda kc: g.modT[:, l, cond, kc:kc + 1])

    def proj(ti, side=None):
        t0, W = TILES[ti]
        hh, bh = hh2[ti % 2], bh2[ti % 2]
        u_t, bu = ut[ti % 2], but[ti % 2]
        for n in range(23):
            M = 128 if n < 22 else 64
            wb = sorted(set([(n * 128) // 960, (n * 128 + M - 1) // 960]))
            pt, bp = up.next()
            for kc in range(8):
                fw.op("pe", lambda h: h.matmul(pt[:M, :W], win[:, kc, n * 128:n * 128 + M], hh[:, kc, :W],
                                               start=(kc == 0), stop=(kc == 7)),
                      reads=[bwin[i] for i in wb] + [bh[kc]], writes=[bp], inc=(kc == 7))
            if n % 2 == 0:
                fw.op("act", lambda h: h.activation(out=u_t[:M, n, :W], in_=pt[:M, :W], func=AF.Copy), reads=[bp], writes=[bu[n]])
            else:
                fw.op("dve", lambda h: h.tensor_copy(out=u_t[:M, n, :W], in_=pt[:M, :W]), reads=[bp], writes=[bu[n]])
            if side is not None and n >= 2:
                next(side, None)
        if side is not None:
            for _ in side:
                pass
        fw.dma("sp", g.UT[0:2816, t0:t0 + W].rearrange("(c p) t -> p c t", p=128), u_t[:, 0:22, :W], reads=bu[0:22], sbuf=bu[0])
        fw.dma("sp", g.UT[2816:2880, t0:t0 + W], u_t[:64, 22, :W], reads=[bu[22]], sbuf=bu[22])

    load(0)
    load(1)
    for _ in norm(0):
        pass
    for ti in range(NT):
        if ti + 2 < NT:
            load(ti + 2)
        proj(ti, norm(ti + 1) if ti + 1 < NT else None)
    ph.close()


def phase_outproj(fw, g, l, xsrc, last):
    ph = Phase(fw)
    wo = ph.sb([128, 8, D], BF16)
    bwo = Buf()
    cast_load3(fw, wo[:, :, :], g.w_out[l, :, :], bwo)
    xt = ph.ring(2, [128, 8, 512], F32)
    yc = ph.ring(2, [128, 8, 512], BF16)
    xn = [ph.sb([128, 8, 512], F32) for _ in range(2)]
    bxn = [Buf() for _ in range(2)]
    nb = norm_bufs(ph)
    hh = [ph.sb([128, 8, 512], BF16) for _ in range(2)]
    bhh = [[Buf() for _ in range(8)] for _ in range(2)]
    op = ph.ring(3, [128, 512], F32, psum=True)
    tiles = list(enumerate(TILES))
    if last:
        tiles = tiles[1:]
    loaded = {}

    def load(ti):
        t0, W = TILES[ti]
        x_t, bx = xt.next()
        y_t, by = yc.next()
        fw.dma("sp", x_t[:, :, :W], xsrc[:, t0:t0 + W].rearrange("(kc p) t -> p kc t", p=128), writes=[bx], sbuf=bx)
        fw.dma("sp", y_t[:, :, :W], g.YC[:, t0:t0 + W].rearrange("(kc p) t -> p kc t", p=128), writes=[by], sbuf=by)
        loaded[ti] = (x_t, bx, y_t, by)

    def norm_side(i, ti, t0, W, cond, xo, bxo):
        h_t, bh = hh[i % 2], bhh[i % 2]
        gen = norm_modulate_gen(fw, g, nb, xo, bxo, h_t, bh, W,
                                lambda kc: g.modT[:, l, cond, 32 + kc:33 + kc], lambda kc: g.modT[:, l, cond, 24 + kc:25 + kc])

        def fin():
            for _ in gen:
                pass
            fw.dma("sp", g.H2[:, t0:t0 + W].rearrange("(kc p) t -> p kc t", p=128), h_t[:, :, :W], reads=bh, sbuf=bh[0])
        return gen, fin

    load(tiles[0][0])
    side = None
    for i, (ti, (t0, W)) in enumerate(tiles):
        if i + 1 < len(tiles):
            load(tiles[i + 1][0])
        x_t, bx, y_t, by = loaded.pop(ti)
        cond = 1 if ti == 0 else 0
        xo, bxo = xn[i % 2], bxn[i % 2]
        for n in range(8):
            pt, bp = op.next()
            for kc in range(8):
                fw.op("pe", lambda h: h.matmul(pt[:, :W], wo[:, kc, n * 128:(n + 1) * 128], y_t[:, kc, :W],
                                               start=(kc == 0), stop=(kc == 7)),
                      reads=[bwo, by], writes=[bp], inc=(kc == 7))
            fw.op("dve", lambda h: h.scalar_tensor_tensor(out=xo[:, n, :W], in0=pt[:, :W],
                                                          scalar=g.modT[:, l, cond, 16 + n:17 + n], in1=x_t[:, n, :W],
                                                          op0=ALU.mult, op1=ALU.add),
                  reads=[bp, bx, g.bmod], writes=[bxo])
            if side is not None:
                next(side[0], None)
        fw.dma("sp", g.XR[:, t0:t0 + W].rearrange("(kc p) t -> p kc t", p=128), xo[:, :, :W], reads=[bxo], sbuf=bxo)
        if side is not None:
            side[1]()
        side = norm_side(i, ti, t0, W, cond, xo, bxo)
    side[1]()
    ph.close()


def phase_mlp(fw, g, l, last):
    ph = Phase(fw)
    w1 = ph.sb([128, 8, 4096], BF16)
    w2 = ph.sb([128, 32, D], BF16)
    bw1 = [Buf() for _ in range(4)]
    bw2 = [Buf() for _ in range(4)]
    for cb in range(4):
        cast_load3(fw, w1[:, :, cb * 1024:(cb + 1) * 1024], g.w1[l, :, cb * 1024:(cb + 1) * 1024], bw1[cb])
    for fb in range(4):
        cast_load3(fw, w2[:, fb * 8:(fb + 1) * 8, :], g.w2[l, fb * 1024:(fb + 1) * 1024, :], bw2[fb])
    h2 = ph.ring(2, [128, 8, 512], BF16)
    hid = ph.sb([128, 32, 512], BF16)
    bhid = [Buf() for _ in range(32)]
    rr = ph.ring(2, [128, 512], F32)
    xc = ph.ring(4, [128, 512], F32)
    xo = ph.ring(4, [128, 512], F32)
    pp = ph.ring(8, [128, 512], F32, psum=True)
    GI = 4
    tiles = list(enumerate(TILES))
    if last:
        tiles = tiles[1:]
    loaded = {}

    def load(ti):
        t0, W = TILES[ti]
        h_t, bh = h2.next()
        fw.dma("sp", h_t[:, :, :W], g.H2[:, t0:t0 + W].rearrange("(kc p) t -> p kc t", p=128), writes=[bh], sbuf=bh)
        loaded[ti] = (h_t, bh)

    load(tiles[0][0])
    for i, (ti, (t0, W)) in enumerate(tiles):
        if i + 1 < len(tiles):
            load(tiles[i + 1][0])
        h_t, bh = loaded.pop(ti)
        cond = 1 if ti == 0 else 0
        for fg in range(32 // GI):
            pts = [pp.next() for _ in range(GI)]
            for kc in range(8):
                for j in range(GI):
                    f = fg * GI + j
                    pt, bp = pts[j]
                    fw.op("pe", lambda h: h.matmul(pt[:, :W], w1[:, kc, f * 128:(f + 1) * 128], h_t[:, kc, :W],
                                                   start=(kc == 0), stop=(kc == 7)),
                          reads=[bw1[f // 8], bh], writes=[bp], inc=(kc == 7))
            for j in range(GI):
                f = fg * GI + j
                pt, bp = pts[j]
                r_t, br = rr.next()
                fw.op("act", lambda h: h.activation(out=r_t[:, :W], in_=pt[:, :W], func=AF.Relu), reads=[bp], writes=[br])
                fw.op("dve", lambda h: h.tensor_tensor(out=hid[:, f, :W], in0=r_t[:, :W], in1=r_t[:, :W], op=ALU.mult),
                      reads=[br], writes=[bhid[f]])
        for ng in range(8 // GI):
            pts = [pp.next() for _ in range(GI)]
            xcs = []
            for j in range(GI):
                n = ng * GI + j
                x_c, bxc = xc.next()
                fw.dma("sp", x_c[:, :W], g.XR[n * 128:(n + 1) * 128, t0:t0 + W], writes=[bxc], sbuf=bxc)
                xcs.append((x_c, bxc))
            for f in range(32):
                for j in range(GI):
                    n = ng * GI + j
                    pt, bp = pts[j]
                    fw.op("pe", lambda h: h.matmul(pt[:, :W], w2[:, f, n * 128:(n + 1) * 128], hid[:, f, :W],
                                                   start=(f == 0), stop=(f == 31)),
                          reads=[bw2[f // 8], bhid[f]], writes=[bp], inc=(f == 31))
            for j in range(GI):
                n = ng * GI + j
                pt, bp = pts[j]
                x_c, bxc = xcs[j]
                x_o, bxo = xo.next()
                fw.op("dve", lambda h: h.scalar_tensor_tensor(out=x_o[:, :W], in0=pt[:, :W],
                                                              scalar=g.modT[:, l, cond, 40 + n:41 + n], in1=x_c[:, :W],
                                                              op0=ALU.mult, op1=ALU.add),
                      reads=[bp, bxc, g.bmod], writes=[bxo])
                fw.dma("sp", g.XR[n * 128:(n + 1) * 128, t0:t0 + W], x_o[:, :W], reads=[bxo], sbuf=bxo)
    ph.close()


def phase_final(fw, g):
    ph = Phase(fw)
    xt = ph.ring(2, [128, 8, 512], F32)
    nb = norm_bufs(ph)
    sq, bsq, ssp, bssp, sd, bsd, rstd, brstd, tt, btt = nb
    oo = [ph.sb([128, 8, 512], F32) for _ in range(2)]
    boo = [[Buf() for _ in range(8)] for _ in range(2)]
    ftiles = TILES[1:]

    def fload(i):
        t0, W = ftiles[i]
        x_t, bx = xt.next()
        fw.dma("sp", x_t[:, :, :W], g.XR[:, t0:t0 + W].rearrange("(kc p) t -> p kc t", p=128), writes=[bx], sbuf=bx)
        return (x_t, bx)

    nxt = fload(0)
    for i, (t0, W) in enumerate(ftiles):
        x_t, bx = nxt
        if i + 1 < len(ftiles):
            nxt = fload(i + 1)
        fw.op("act", lambda h: h.activation(out=sq[:, :, :W], in_=x_t[:, :, :W], func=AF.Square), reads=[bx], writes=[bsq])
        rstd_from_sq(fw, g, [sq[:, kc, :W] for kc in range(8)], bsq, g.ones_bf[:, :], ssp, bssp, sd, bsd, rstd, brstd, W, D)
        o_t, bo = oo[i % 2], boo[i % 2]
        for kc in range(8):
            fw.op("dve", lambda h: h.scalar_tensor_tensor(out=o_t[:, kc, :W], in0=x_t[:, kc, :W], scalar=g.fing[:, kc:kc + 1],
                                                          in1=rstd[:, :W], op0=ALU.mult, op1=ALU.mult),
                  reads=[bx, brstd, g.bconst], writes=[bo[kc]])
        fw.dma("sp", g.out[:, t0 - CT:t0 - CT + W].rearrange("(kc p) t -> p kc t", p=128), o_t[:, :, :W], reads=bo, sbuf=bo[0])
    ph.close()


R_AV, R_AG, R_CQ, R_CKV, R_QA, R_QB, R_KA, R_KB, R_V = 0, 256, 512, 768, 896, 1152, 1408, 1536, 1664
R_RQ, R_RK, R_RV, R_GF, R_GB, R_KPA, R_KPB = 1792, 1920, 2048, 2304, 2560, 2816, 2848


def gen_conv(fw, g, ph, pp, l, ctx_out):
    dw = ph.sb([128, 2, 31], F32)
    bdw = Buf()
    fw.dma("sp", dw[:], g.conv_dwT[l, :, :, :], writes=[bdw], sbuf=bdw)
    pw = ph.sb([128, 2, 256], BF16)
    bpw = Buf()
    cast_load3(fw, pw[:, :, :], g.conv_pw[l, :, :], bpw)
    dg = ph.sb([128, 2, 31, 128], BF16)
    bdg = Buf()
    for c in range(2):
        for j in range(31):
            fw.op("dve", lambda h: h.tensor_scalar(out=dg[:, c, j, :], in0=g.ident_bf[:, :], scalar1=dw[:, c, j:j + 1],
                                                   scalar2=None, op0=ALU.mult), reads=[bdw, g.bconst], writes=[bdg])
    ypad = ph.sb([128, 2, L + 30], BF16)
    byp = Buf()
    va = ph.ring(2, [128, 2, 512], BF16)
    ga = ph.ring(2, [128, 2, 512], BF16)
    sgm = ph.ring(2, [128, 2, 512], F32)
    cps = pp
    z2 = [ph.sb([128, 2, 512], F32) for _ in range(2)]
    zs2 = [ph.sb([128, 2, 512], F32) for _ in range(2)]
    bz2 = [[Buf(), Buf()] for _ in range(2)]
    bzs2 = [[Buf(), Buf()] for _ in range(2)]
    op = pp
    mean, msq, var, rstd = [ph.sb([128, 512], F32) for _ in range(4)]
    bmean, bmsq, bvar, brstd = [Buf() for _ in range(4)]
    sd, bsd = rstd, brstd
    dd = ph.sb([128, 2, 512], F32)
    d2 = ph.sb([128, 2, 512], F32)
    bdd = [Buf(), Buf()]
    bd2 = [Buf(), Buf()]
    ss = ph.sb([128, 2, 512], BF16)
    bss = [Buf(), Buf()]
    yo = [ph.sb([128, 2, 512], BF16) for _ in range(2)]
    byo = [[Buf(), Buf()] for _ in range(2)]
    segs = ([(0, CT)] if ctx_out else []) + [(CT, L)]
    it = 0
    for (s0, SL) in segs:
        for c in range(2):
            fw.op("dve", lambda h: h.memset(ypad[:, c, 0:15], 0.0), writes=[byp])
            fw.op("dve", lambda h: h.memset(ypad[:, c, 15 + SL:30 + SL], 0.0), writes=[byp])
        for t0 in range(0, SL, 512):
            W = min(512, SL - t0)
            v_t, bv = va.next()
            g_t, bg = ga.next()
            s_t, bs = sgm.next()
            fw.dma("sp", v_t[:, :, :W], g.UT[R_AV:R_AV + 256, s0 + t0:s0 + t0 + W].rearrange("(c p) t -> p c t", p=128),
                   writes=[bv], sbuf=bv)
            fw.dma("sp", g_t[:, :, :W], g.UT[R_AG:R_AG + 256, s0 + t0:s0 + t0 + W].rearrange("(c p) t -> p c t", p=128),
                   writes=[bg], sbuf=bg)
            fw.op("act", lambda h: h.activation(out=s_t[:, :, :W], in_=g_t[:, :, :W], func=AF.Sigmoid), reads=[bg], writes=[bs])
            fw.op("dve", lambda h: h.tensor_tensor(out=ypad[:, :, 15 + t0:15 + t0 + W], in0=v_t[:, :, :W], in1=s_t[:, :, :W],
                                                   op=ALU.mult), reads=[bv, bs], writes=[byp])
            if (t0 // 512) % 4 == 3:
                yield
        for t0 in range(0, SL, 512):
            W = min(512, SL - t0)
            z, zs, bz, bzs = z2[it % 2], zs2[it % 2], bz2[it % 2], bzs2[it % 2]
            for c in range(2):
                cp, bcp = cps.next()
                for j in range(31):
                    fw.op("pe", lambda h: h.matmul(cp[:, :W], dg[:, c, j, :], ypad[:, c, t0 + j:t0 + j + W],
                                                   start=(j == 0), stop=(j == 30)),
                          reads=[bdg, byp], writes=[bcp], inc=(j == 30))
                fw.op("act", lambda h: h.activation(out=z[:, c, :W], in_=cp[:, :W], func=AF.Identity,
                                                    bias=g.vecs[:, l, c:c + 1], scale=1.0),
                      reads=[bcp, g.bconst], writes=[bz[c]])
                fw.op("act", lambda h: h.activation(out=zs[:, c, :W], in_=z[:, c, :W], func=AF.Square),
                      reads=[bz[c]], writes=[bzs[c]])
            mp, bmp = pp.next()
            sp_, bsp = pp.next()
            for c in range(2):
                fw.op("pe", lambda h: h.matmul(mp[:, :W], g.ones_f[:, :], z[:, c, :W], start=(c == 0), stop=(c == 1)),
                      reads=[bz[c], g.bconst], writes=[bmp], inc=(c == 1))
            for c in range(2):
                fw.op("pe", lambda h: h.matmul(sp_[:, :W], g.ones_f[:, :], zs[:, c, :W], start=(c == 0), stop=(c == 1)),
                      reads=[bzs[c], g.bconst], writes=[bsp], inc=(c == 1))
            fw.op("dve", lambda h: h.tensor_scalar(out=mean[:, :W], in0=mp[:, :W], scalar1=1.0 / 256, scalar2=None, op0=ALU.mult),
                  reads=[bmp], writes=[bmean])
            fw.op("dve", lambda h: h.tensor_tensor(out=msq[:, :W], in0=mean[:, :W], in1=mean[:, :W], op=ALU.mult),
                  reads=[bmean], writes=[bmsq])
            fw.op("dve", lambda h: h.scalar_tensor_tensor(out=var[:, :W], in0=sp_[:, :W], scalar=1.0 / 256, in1=msq[:, :W],
                                                          op0=ALU.mult, op1=ALU.subtract), reads=[bsp, bmsq], writes=[bvar])
            fw.op("act", lambda h: h.activation(out=sd[:, :W], in_=var[:, :W], func=AF.Ln, bias=g.epsb[:, :], scale=1.0),
                  reads=[bvar, g.bconst], writes=[bsd])
            fw.op("act", lambda h: h.activation(out=rstd[:, :W], in_=sd[:, :W], func=AF.Exp, scale=-0.5), reads=[bsd], writes=[brstd])
            for c in range(2):
                fw.op("dve", lambda h: h.tensor_tensor(out=dd[:, c, :W], in0=z[:, c, :W], in1=mean[:, :W], op=ALU.subtract),
                      reads=[bz[c], bmean], writes=[bdd[c]])
                fw.op("dve", lambda h: h.tensor_tensor(out=d2[:, c, :W], in0=dd[:, c, :W], in1=rstd[:, :W], op=ALU.mult),
                      reads=[bdd[c], brstd], writes=[bd2[c]])
                fw.op("act", lambda h: h.activation(out=ss[:, c, :W], in_=d2[:, c, :W], func=AF.Silu,
                                                    bias=g.vecs[:, l, 4 + c:5 + c], scale=g.vecs[:, l, 2 + c:3 + c]),
                      reads=[bd2[c], g.bconst], writes=[bss[c]])
            y_t, by = yo[it % 2], byo[it % 2]
            it += 1
            for co in range(2):
                pt, bp = op.next()
                for ci in range(2):
                    fw.op("pe", lambda h: h.matmul(pt[:, :W], pw[:, ci, co * 128:(co + 1) * 128], ss[:, ci, :W],
                                                   start=(ci == 0), stop=(ci == 1)),
                          reads=[bpw, bss[ci]], writes=[bp], inc=(ci == 1))
                fw.op("act", lambda h: h.activation(out=y_t[:, co, :W], in_=pt[:, :W], func=AF.Copy), reads=[bp], writes=[by[co]])
            fw.dma("sp", g.YC[0:256, s0 + t0:s0 + t0 + W].rearrange("(c p) t -> p c t", p=128), y_t[:, :, :W],
                   reads=by, sbuf=by[0])
            yield


def gen_mla_prep(fw, g, ph, pp, l):
    uqA = ph.sb([128, 2, 384], BF16)
    uqB = ph.sb([128, 2, 384], BF16)
    ukv = ph.sb([128, 4, 128], BF16)
    bw = Buf()
    cast_load3(fw, uqA[:, :, :], g.uqA[l, :, :], bw)
    cast_load3(fw, uqB[:, :, :], g.uqB[l, :, :], bw)
    cast_load(fw, ukv[:].rearrange("p a b -> p (a b)"), g.ukv[l, :, :], bw, 512)
    cq = ph.ring(2, [128, 2, 512], BF16)
    ckv = ph.ring(2, [128, 512], BF16)
    kpa = ph.ring(2, [32, 512], BF16)
    kpb = ph.ring(2, [32, 512], BF16)
    c96 = ph.ring(2, [96, 512], F32)
    s96 = ph.ring(2, [96, 512], F32)
    c32 = ph.ring(2, [32, 512], F32)
    s32 = ph.ring(2, [32, 512], F32)
    sq = ph.sb([128, 3, 512], BF16)
    bsq, bsq2 = Buf(), Buf()
    sd, rstd, sd2, rstd2 = [ph.sb([128, 512], F32) for _ in range(4)]
    bsd, brstd, bsd2, brstd2 = [Buf() for _ in range(4)]
    cqn = ph.sb([128, 2, 512], BF16)
    bcqn = [Buf(), Buf()]
    ckn = ph.sb([128, 512], BF16)
    bckn = Buf()
    pa = pp
    pb = pp
    t1 = ph.ring(2, [96, 512], F32)
    t2 = ph.ring(2, [96, 512], F32)
    qh = ph.ring(3, [96, 512], BF16)
    kh = ph.ring(3, [64, 512], BF16)
    kr = ph.ring(2, [32, 512], BF16)
    vat = ph.ring(3, [128, 4, 128], BF16)
    for (v_t, bv) in vat.items:
        fw.op("dve", lambda h: h.memset(v_t[:, :, 64:128], 1.0), writes=[bv])
    def load(ti):
        t0, W = TILES[ti]
        cq_t, bcq = cq.next()
        ck_t, bck = ckv.next()
        ka_t, bka = kpa.next()
        kb_t, bkb = kpb.next()
        c96t, bc96 = c96.next()
        s96t, bs96 = s96.next()
        c32t, bc32 = c32.next()
        s32t, bs32 = s32.next()
        fw.dma("sp", cq_t[:, :, :W], g.UT[R_CQ:R_CQ + 256, t0:t0 + W].rearrange("(c p) t -> p c t", p=128), writes=[bcq], sbuf=bcq)
        fw.dma("sp", ck_t[:, :W], g.UT[R_CKV:R_CKV + 128, t0:t0 + W], writes=[bck], sbuf=bck)
        fw.dma("sp", ka_t[:, :W], g.UT[R_KPA:R_KPA + 32, t0:t0 + W], writes=[bka], sbuf=bka)
        fw.dma("sp", kb_t[:, :W], g.UT[R_KPB:R_KPB + 32, t0:t0 + W], writes=[bkb], sbuf=bkb)
        fw.dma("sp", c96t[64:96, :W], g.cosm[:, t0:t0 + W], writes=[bc96], sbuf=bc96)
        fw.dma("sp", s96t[64:96, :W], g.sinm[:, t0:t0 + W], writes=[bs96], sbuf=bs96)
        fw.dma("sp", c32t[:, :W], g.cosm[:, t0:t0 + W], writes=[bc32], sbuf=bc32)
        fw.dma("sp", s32t[:, :W], g.sinm[:, t0:t0 + W], writes=[bs32], sbuf=bs32)
        return (cq_t, bcq, ck_t, bck, ka_t, bka, kb_t, bkb, c96t, bc96, s96t, bs96, c32t, bc32, s32t, bs32)

    nxt = load(0)
    for ti, (t0, W) in enumerate(TILES):
        (cq_t, bcq, ck_t, bck, ka_t, bka, kb_t, bkb, c96t, bc96, s96t, bs96, c32t, bc32, s32t, bs32) = nxt
        if ti + 1 < len(TILES):
            nxt = load(ti + 1)
        ssp, bssp = pp.next()
        ssp2, bssp2 = pp.next()
        fw.op("act", lambda h: h.activation(out=sq[:, 0:2, :W], in_=cq_t[:, :, :W], func=AF.Square), reads=[bcq], writes=[bsq])
        fw.op("act", lambda h: h.activation(out=sq[:, 2, :W], in_=ck_t[:, :W], func=AF.Square), reads=[bck], writes=[bsq2])
        rstd_from_sq(fw, g, [sq[:, 0, :W], sq[:, 1, :W]], bsq, g.ones_bf[:, :], ssp, bssp, sd, bsd, rstd, brstd, W, 256)
        rstd_from_sq(fw, g, [sq[:, 2, :W]], bsq2, g.ones_bf[:, :], ssp2, bssp2, sd2, bsd2, rstd2, brstd2, W, 128)
        for kc in range(2):
            fw.op("dve", lambda h: h.scalar_tensor_tensor(out=cqn[:, kc, :W], in0=cq_t[:, kc, :W], scalar=g.vecs[:, l, 6 + kc:7 + kc],
                                                          in1=rstd[:, :W], op0=ALU.mult, op1=ALU.mult),
                  reads=[bcq, brstd, g.bconst], writes=[bcqn[kc]])
        fw.op("dve", lambda h: h.scalar_tensor_tensor(out=ckn[:, :W], in0=ck_t[:, :W], scalar=g.vecs[:, l, 8:9],
                                                      in1=rstd2[:, :W], op0=ALU.mult, op1=ALU.mult),
              reads=[bck, brstd2, g.bconst], writes=[bckn])
        t1t, bt1 = t1.next()
        t2t, bt2 = t2.next()
        kr_t, bkr = kr.next()
        fw.op("dve", lambda h: h.tensor_tensor(out=t1t[:32, :W], in0=ka_t[:, :W], in1=c32t[:, :W], op=ALU.mult), reads=[bka, bc32], writes=[bt1])
        fw.op("dve", lambda h: h.tensor_tensor(out=t2t[:32, :W], in0=kb_t[:, :W], in1=s32t[:, :W], op=ALU.mult), reads=[bkb, bs32], writes=[bt2])
        fw.op("dve", lambda h: h.tensor_tensor(out=kr_t[:, :W], in0=t1t[:32, :W], in1=t2t[:32, :W], op=ALU.add), reads=[bt1, bt2], writes=[bkr])
        for hd in range(4):
            fw.dma("sp", g.KTm[hd, 64:96, t0:t0 + W], kr_t[:, :W], reads=[bkr], sbuf=bkr)
        for hd in range(4):
            pat, bpa = pa.next()
            pbt, bpb = pb.next()
            for kc in range(2):
                fw.op("pe", lambda h: h.matmul(pat[:96, :W], uqA[:, kc, hd * 96:(hd + 1) * 96], cqn[:, kc, :W], start=(kc == 0), stop=(kc == 1)),
                      reads=[bw, bcqn[kc]], writes=[bpa], inc=(kc == 1))
            for kc in range(2):
                fw.op("pe", lambda h: h.matmul(pbt[:96, :W], uqB[:, kc, hd * 96:(hd + 1) * 96], cqn[:, kc, :W], start=(kc == 0), stop=(kc == 1)),
                      reads=[bw, bcqn[kc]], writes=[bpb], inc=(kc == 1))
            q_t, bq = qh.next()
            t1t, bt1 = t1.next()
            t2t, bt2 = t2.next()
            fw.op("act", lambda h: h.activation(out=q_t[0:64, :W], in_=pat[0:64, :W], func=AF.Copy), reads=[bpa], writes=[bq])
            fw.op("dve", lambda h: h.tensor_tensor(out=t1t[64:96, :W], in0=pat[64:96, :W], in1=c96t[64:96, :W], op=ALU.mult), reads=[bpa, bc96], writes=[bt1])
            fw.op("dve", lambda h: h.tensor_tensor(out=t2t[64:96, :W], in0=pbt[64:96, :W], in1=s96t[64:96, :W], op=ALU.mult), reads=[bpb, bs96], writes=[bt2])
            fw.op("dve", lambda h: h.tensor_tensor(out=q_t[64:96, :W], in0=t1t[64:96, :W], in1=t2t[64:96, :W], op=ALU.add), reads=[bt1, bt2], writes=[bq])
            fw.dma("sp", g.QTm[hd, :, t0:t0 + W], q_t[:, :W], reads=[bq], sbuf=bq)
            pat, bpa = pa.next()
            fw.op("pe", lambda h: h.matmul(pat[:64, :W], ukv[:, hd, 0:64], ckn[:, :W], start=True, stop=True), reads=[bw, bckn], writes=[bpa])
            k_t, bk = kh.next()
            fw.op("act", lambda h: h.activation(out=k_t[:, :W], in_=pat[:64, :W], func=AF.Copy), reads=[bpa], writes=[bk])
            fw.dma("sp", g.KTm[hd, 0:64, t0:t0 + W], k_t[:, :W], reads=[bk], sbuf=bk)
        for sub in range(W // 128):
            vpt_, bvp = pp.next()
            vpt = vpt_[:, 0:256].rearrange("p (a b) -> p a b", a=4)
            fw.op("pe", lambda h: h.matmul(vpt[:, :, :], ckn[:, sub * 128:(sub + 1) * 128], ukv[:, :, 64:128], start=True, stop=True),
                  reads=[bw, bckn], writes=[bvp])
            v_t, bv = vat.next()
            fw.op("dve", lambda h: h.tensor_copy(out=v_t[:, :, 0:64], in_=vpt[:, :, :]), reads=[bvp], writes=[bv])
            r0 = t0 + sub * 128
            fw.dma("sp", g.VAm[r0:r0 + 128, :].rearrange("p (a b) -> p a b", a=4), v_t[:, :, :], reads=[bv], sbuf=bv)
        yield


def gen_gqa_prep(fw, g, ph, pp, l):
    xa = ph.ring(2, [128, 3, 512], BF16)
    xb = ph.ring(2, [128, 3, 512], BF16)
    vv = ph.ring(2, [128, 512], BF16)
    cs = ph.ring(2, [128, 512], F32)
    sn = ph.ring(2, [128, 512], F32)
    sq = ph.ring(2, [128, 512], BF16)
    ssp = pp
    sd = ph.ring(2, [128, 512], F32)
    rstd = ph.ring(2, [128, 512], F32)
    t1 = ph.ring(2, [128, 512], F32)
    t2 = ph.ring(2, [128, 512], F32)
    t3 = ph.ring(2, [128, 512], F32)
    qr = ph.ring(3, [128, 512], BF16)
    vat = ph.ring(3, [128, 2, 128], BF16)
    for (v_t, bv) in vat.items:
        fw.op("dve", lambda h: h.memset(v_t[:, :, 64:128], 1.0), writes=[bv])
    def load(ti):
        t0, W = TILES[ti]
        xa_t, bxa = xa.next()
        xb_t, bxb = xb.next()
        v_in, bvi = vv.next()
        cs_t, bcs = cs.next()
        sn_t, bsn = sn.next()
        fw.dma("sp", xa_t[:, 0:2, :W], g.UT[R_QA:R_QA + 256, t0:t0 + W].rearrange("(c p) t -> p c t", p=128), writes=[bxa], sbuf=bxa)
        fw.dma("sp", xa_t[:, 2, :W], g.UT[R_KA:R_KA + 128, t0:t0 + W], writes=[bxa], sbuf=bxa, nodeps=True)
        fw.dma("sp", xb_t[:, 0:2, :W], g.UT[R_QB:R_QB + 256, t0:t0 + W].rearrange("(c p) t -> p c t", p=128), writes=[bxb], sbuf=bxb)
        fw.dma("sp", xb_t[:, 2, :W], g.UT[R_KB:R_KB + 128, t0:t0 + W], writes=[bxb], sbuf=bxb, nodeps=True)
        fw.dma("sp", v_in[:, :W], g.UT[R_V:R_V + 128, t0:t0 + W], writes=[bvi], sbuf=bvi)
        fw.dma("sp", cs_t[:, :W], g.cosg[:, t0:t0 + W], writes=[bcs], sbuf=bcs)
        fw.dma("sp", sn_t[:, :W], g.sing[:, t0:t0 + W], writes=[bsn], sbuf=bsn)
        return (xa_t, bxa, xb_t, bxb, v_in, bvi, cs_t, bcs, sn_t, bsn)

    nxt = load(0)
    for ti, (t0, W) in enumerate(TILES):
        (xa_t, bxa, xb_t, bxb, v_in, bvi, cs_t, bcs, sn_t, bsn) = nxt
        if ti + 1 < len(TILES):
            nxt = load(ti + 1)
        for c in range(3):
            ga, gb = (9, 10) if c < 2 else (11, 12)
            sq_t, bsq = sq.next()
            ss_t, bss = ssp.next()
            sd_t, bsd = sd.next()
            rs_t, brs = rstd.next()
            fw.op("act", lambda h: h.activation(out=sq_t[:, :W], in_=xa_t[:, c, :W], func=AF.Square), reads=[bxa], writes=[bsq])
            rstd_from_sq(fw, g, [sq_t[:, :W]], bsq, g.blk_bf[:, :], ss_t, bss, sd_t, bsd, rs_t, brs, W, 64)
            a_t, ba = t1.next()
            b_t, bb = t2.next()
            c_t, bc = t3.next()
            q_t, bq = qr.next()
            fw.op("dve", lambda h: h.scalar_tensor_tensor(out=a_t[:, :W], in0=xa_t[:, c, :W], scalar=g.vecs[:, l, ga:ga + 1], in1=cs_t[:, :W],
                                                          op0=ALU.mult, op1=ALU.mult), reads=[bxa, bcs, g.bconst], writes=[ba])
            fw.op("dve", lambda h: h.scalar_tensor_tensor(out=b_t[:, :W], in0=xb_t[:, c, :W], scalar=g.vecs[:, l, gb:gb + 1], in1=sn_t[:, :W],
                                                          op0=ALU.mult, op1=ALU.mult), reads=[bxb, bsn, g.bconst], writes=[bb])
            fw.op("pool", lambda h: h.tensor_tensor(out=c_t[:, :W], in0=a_t[:, :W], in1=b_t[:, :W], op=ALU.add), reads=[ba, bb], writes=[bc])
            fw.op("pool", lambda h: h.tensor_tensor(out=q_t[:, :W], in0=c_t[:, :W], in1=rs_t[:, :W], op=ALU.mult), reads=[bc, brs], writes=[bq])
            if c < 2:
                fw.dma("sp", g.QTg[c, :, t0:t0 + W], q_t[0:64, :W], reads=[bq], sbuf=bq)
                fw.dma("sp", g.QTg[c + 2, :, t0:t0 + W], q_t[64:128, :W], reads=[bq], sbuf=bq)
            else:
                fw.dma("sp", g.KTg[0, :, t0:t0 + W], q_t[0:64, :W], reads=[bq], sbuf=bq)
                fw.dma("sp", g.KTg[1, :, t0:t0 + W], q_t[64:128, :W], reads=[bq], sbuf=bq)
        for sub in range(W // 128):
            vpt_, bvp = pp.next()
            vpt = vpt_[:, 0:128].rearrange("p (a b) -> p a b", a=2)
            fw.op("pe", lambda h: h.matmul(vpt[:, :, :], v_in[:, sub * 128:(sub + 1) * 128], g.ident_bf[:, :], start=True, stop=True),
                  reads=[bvi, g.bconst], writes=[bvp])
            v_t, bv = vat.next()
            fw.op("dve", lambda h: h.tensor_copy(out=v_t[:, :, 0:64], in_=vpt[:, :, :]), reads=[bvp], writes=[bv])
            r0 = t0 + sub * 128
            fw.dma("sp", g.VAg[r0:r0 + 128, :].rearrange("p (a b) -> p a b", a=2), v_t[:, :, :], reads=[bv], sbuf=bv)
        yield


def phase_prep(fw, g, l, ctx_out):
    ph = Phase(fw)
    pp = ph.ring(8, [128, 512], F32, psum=True)
    gens = [gen_conv(fw, g, ph, pp, l, ctx_out), gen_mla_prep(fw, g, ph, pp, l), gen_gqa_prep(fw, g, ph, pp, l)]
    while gens:
        for gn in list(gens):
            try:
                next(gn)
            except StopIteration:
                gens.remove(gn)
    ph.close()


def phase_attn(fw, g, QT, KT, VA, nheads, nkv, dk, scale, yrow0, ctx_out):
    ph = Phase(fw)
    pad = (dk == 64)
    if pad:
        qT = [ph.sb([128, T], BF16) for _ in range(2)]
        kT = [ph.sb([128, T], BF16) for _ in range(nkv)]
        bq = [Buf() for _ in range(2)]
        bk = [Buf() for _ in range(nkv)]
        for kv in range(nkv):
            fw.op("pool", lambda h: h.memset(kT[kv][(1 - kv) * 64:(2 - kv) * 64, :], 0.0), writes=[bk[kv]])
            fw.dma("sp", kT[kv][kv * 64:(kv + 1) * 64, :], KT[kv, :, :], writes=[bk[kv]], sbuf=bk[kv], nodeps=True)
        for c in range(2):
            fw.dma("sp", qT[c][0:64, :], QT[c, :, :], writes=[bq[c]], sbuf=bq[c])
            fw.dma("sp", qT[c][64:128, :], QT[c + 2, :, :], writes=[bq[c]], sbuf=bq[c], nodeps=True)
        KD = 128
    else:
        qT = [ph.sb([dk, T], BF16) for _ in range(nheads)]
        kT = [ph.sb([dk, T], BF16) for _ in range(nkv)]
        bq = [Buf() for _ in range(nheads)]
        bk = [Buf() for _ in range(nkv)]
        fw.dma("sp", kT[0][:, :], KT[0, :, :], writes=[bk[0]], sbuf=bk[0])
        fw.dma("sp", qT[0][:, :], QT[0, :, :], writes=[bq[0]], sbuf=bq[0])
        KD = dk
    va = ph.sb([128, NKT, nkv * 128], BF16)
    bva = Buf()
    fw.dma("sp", va[:, :, :], VA[:, :].rearrange("(k p) c -> p k c", p=128), writes=[bva], sbuf=bva)
    if not pad:
        for kv in range(1, nkv):
            fw.dma("sp", kT[kv][:, :], KT[kv, :, :], writes=[bk[kv]], sbuf=bk[kv])
        for hd in range(1, nheads):
            fw.dma("sp", qT[hd][:, :], QT[hd, :, :], writes=[bq[hd]], sbuf=bq[hd])
    s_ps = ph.ring(3, [128, 2, 512], F32, psum=True)
    p_sb = ph.ring(4, [128, 2, 512], BF16)
    o_ps = ph.ring(1, [128, 512], F32, psum=True)
    bc_ps = ph.ps()
    bbc = Buf()
    o_sb = ph.ring(2, [128, 512], F32)
    rbc = ph.sb([64, 512], F32)
    brbc = Buf()
    yh = ph.ring(2, [64, 512], BF16)
    steps = []
    for hd in range(nheads):
        kv = hd * nkv // nheads
        for ti, (t0, W) in enumerate(TILES):
            if ti == 0 and not ctx_out:
                continue
            nk = 2 if ti == 0 else NKT
            for kp in range(nk // 2):
                steps.append((hd, kv, t0, W, kp, nk // 2))
    LA = 2
    deferred = []
    pend = {}
    cur_o = [None]

    def emit_s(i):
        hd, kv, t0, W, kp, npair = steps[i]
        qi = (hd % 2) if pad else hd
        st, bs = s_ps.next()
        for j in range(2):
            kt = 2 * kp + j
            fw.op("pe", lambda h: h.matmul(st[:, j, :W], kT[kv][:KD, kt * 128:(kt + 1) * 128], qT[qi][:KD, t0:t0 + W], start=True, stop=True),
                  reads=[bk[kv], bq[qi]], writes=[bs], inc=(j == 1))
        pt, bp = p_sb.next()
        fw.op("act", lambda h: h.activation(out=pt[:, :, :W], in_=st[:, :, :W], func=AF.Exp, scale=scale), reads=[bs], writes=[bp])
        pend[i] = (pt, bp)

    def emit_pv(i):
        hd, kv, t0, W, kp, npair = steps[i]
        pt, bp = pend.pop(i)
        if kp == 0:
            cur_o[0] = o_ps.next()
        ot, bo = cur_o[0]
        for j in range(2):
            kt = 2 * kp + j
            first = (kp == 0 and j == 0)
            lastm = (kp == npair - 1 and j == 1)
            fw.op("pe", lambda h: h.matmul(ot[:, :W], va[:, kt, kv * 128:(kv + 1) * 128], pt[:, j, :W], start=first, stop=lastm),
                  reads=[bva, bp], writes=[bo], inc=lastm)
        if kp == npair - 1:
            os_t, bos = o_sb.next()
            fw.op("dve", lambda h: h.tensor_copy(out=os_t[:65, :W], in_=ot[:65, :W]), reads=[bo], writes=[bos])

            def fin(os_t=os_t, bos=bos, hd=hd, t0=t0, W=W):
                fw.op("pe", lambda h: h.matmul(bc_ps[:64, :W], g.ones_f[64:65, 0:64], os_t[64:65, :W], start=True, stop=True),
                      reads=[bos, g.bconst], writes=[bbc])
                fw.op("dve", lambda h: h.reciprocal(out=rbc[:, :W], in_=bc_ps[:64, :W]), reads=[bbc], writes=[brbc])
                y_t, by = yh.next()
                fw.op("dve", lambda h: h.tensor_tensor(out=y_t[:, :W], in0=os_t[:64, :W], in1=rbc[:, :W], op=ALU.mult), reads=[bos, brbc], writes=[by])
                fw.dma("sp", g.YC[yrow0 + hd * 64:yrow0 + (hd + 1) * 64, t0:t0 + W], y_t[:, :W], reads=[by], sbuf=by)
            deferred.append(fin)

    n = len(steps)
    for i in range(n + LA):
        todo = list(deferred)
        del deferred[:]
        if i < n:
            emit_s(i)
        if i >= LA:
            emit_pv(i - LA)
        for f in todo:
            f()
    for f in deferred:
        f()
    ph.close()


def phase_ret_prep(fw, g, l, mod_next=None):
    ph = Phase(fw)
    e1 = ph.sb([128, 8], F32)
    lg = ph.sb([128, 8], F32)
    be1, blg = Buf(), Buf()
    fw.op("act", lambda h: h.activation(out=e1[:, :], in_=g.vecs[:, l, 21:29], func=AF.Exp, scale=-1.0), reads=[g.bconst], writes=[be1])
    fw.op("act", lambda h: h.activation(out=lg[:, :], in_=e1[:, :], func=AF.Ln, bias=g.oneb[:, :], scale=1.0), reads=[be1, g.bconst], writes=[blg])
    fw.op("dve", lambda h: h.tensor_scalar(out=g.lg[:, l, :], in0=lg[:, :], scalar1=-1.0, scalar2=None, op0=ALU.mult), reads=[blg], writes=[g.blg])
    fk = ph.sb([128, 2, 4, 32], F32)
    bfk = Buf()
    tmp = ph.sb([128, 8], F32)
    btmp = Buf()
    for d in range(2):
        src = g.c_rev if d == 0 else g.c_pidx
        for hd in range(4):
            j = d * 4 + hd
            fw.op("act", lambda h: h.activation(out=tmp[:, j:j + 1], in_=src, func=AF.Exp, scale=g.lg[:, l, j:j + 1]),
                  reads=[g.blg, g.bconst], writes=[btmp])
            fw.op("dve", lambda h: h.tensor_scalar(out=fk[:, d, hd, :], in0=g.ones_f[:, 0:32], scalar1=tmp[:, j:j + 1], scalar2=32.0 ** -0.5,
                                                   op0=ALU.mult, op1=ALU.mult), reads=[btmp, g.bconst], writes=[bfk])
    rk = ph.sb([128, T], BF16)
    rv = ph.sb([128, 2, T], BF16)
    brk, brv = Buf(), Buf()
    fw.dma("sp", rk[:, :], g.UT[R_RK:R_RK + 128, :], writes=[brk], sbuf=brk)
    fw.dma("sp", rv[:, :, :], g.UT[R_RV:R_RV + 256, :].rearrange("(c p) t -> p c t", p=128), writes=[brv], sbuf=brv)
    kp = ph.ring(2, [128, 128], F32, psum=True)
    vp = ph.ring(2, [128, 256], F32, psum=True)
    kd = ph.ring(3, [128, 2, 128], BF16)
    vt = ph.ring(3, [128, 256], BF16)
    def chunk(c):
        kpt, bkp = kp.next()
        fw.op("pe", lambda h: h.matmul(kpt[:, :], rk[:, c * 128:(c + 1) * 128], g.ident_bf[:, :], start=True, stop=True),
              reads=[brk, g.bconst], writes=[bkp])
        kd_t, bkd = kd.next()
        for d in range(2):
            fw.op("dve", lambda h: h.tensor_tensor(out=kd_t[:, d, :], in0=kpt[:, :], in1=fk[:, d, :, :].rearrange("p a b -> p (a b)"), op=ALU.mult),
                  reads=[bkp, bfk], writes=[bkd])
        fw.dma("sp", g.KDEC[:, c * 128:(c + 1) * 128, :].rearrange("d p f -> p d f"), kd_t[:, :, :], reads=[bkd], sbuf=bkd)
        vpt, bvp = vp.next()
        for cc in range(2):
            fw.op("pe", lambda h: h.matmul(vpt[:, cc * 128:(cc + 1) * 128], rv[:, cc, c * 128:(c + 1) * 128], g.ident_bf[:, :], start=True, stop=True),
                  reads=[brv, g.bconst], writes=[bvp], inc=(cc == 1))
        v_t, bv = vt.next()
        fw.op("act", lambda h: h.activation(out=v_t[:, :], in_=vpt[:, :], func=AF.Copy), reads=[bvp], writes=[bv])
        fw.dma("sp", g.VTOK[c * 128:(c + 1) * 128, :], v_t[:, :], reads=[bv], sbuf=bv)
    gt = ph.ring(2, [64, T], BF16)
    st = ph.ring(2, [64, T], F32)
    so = ph.ring(2, [64, T], BF16)
    gates = [(d, hd) for d in range(2) for hd in range(4)]
    gl = {}

    def gload(i):
        d, hd = gates[i]
        r0 = R_GF if d == 0 else R_GB
        g_t, bg = gt.next()
        fw.dma("sp", g_t[:, :], g.UT[r0 + hd * 64:r0 + (hd + 1) * 64, :], writes=[bg], sbuf=bg)
        gl[i] = (g_t, bg)

    def gproc(i):
        d, hd = gates[i]
        g_t, bg = gl.pop(i)
        s_t, bs = st.next()
        o_t, bo = so.next()
        fw.op("act", lambda h: h.activation(out=s_t[:, :], in_=g_t[:, :], func=AF.Silu), reads=[bg], writes=[bs])
        fw.op("dve", lambda h: h.tensor_scalar(out=o_t[:, :], in0=s_t[:, :], scalar1=g.vecs[0:64, l, 13 + d * 4 + hd:14 + d * 4 + hd], scalar2=None,
                                               op0=ALU.mult), reads=[bs, g.bconst], writes=[bo])
        fw.dma("sp", g.SG[d, hd, :, :], o_t[:, :], reads=[bo], sbuf=bo)

    gload(0)
    ci = 0
    for i in range(8):
        if i + 1 < 8:
            gload(i + 1)
        gproc(i)
        for _ in range(4):
            if ci < NKT:
                chunk(ci)
                ci += 1
    while ci < NKT:
        chunk(ci)
        ci += 1
    if mod_next is not None:
        emit_mod(fw, g, ph, mod_next)
    ph.close()


def phase_ret_scan(fw, g, l, ctx_out):
    ph = Phase(fw)
    qT = [ph.sb([32, T], BF16) for _ in range(4)]
    kT = [ph.sb([32, T], BF16) for _ in range(4)]
    bq = [Buf() for _ in range(4)]
    bk = [Buf() for _ in range(4)]
    for hd in range(4):
        fw.dma("sp", qT[hd][:, :], g.UT[R_RQ + hd * 32:R_RQ + (hd + 1) * 32, :], writes=[bq[hd]], sbuf=bq[hd])
        fw.dma("sp", kT[hd][:, :], g.UT[R_RK + hd * 32:R_RK + (hd + 1) * 32, :], writes=[bk[hd]], sbuf=bk[hd])
    vtok = ph.sb([128, NKT, 256], BF16)
    kdec = ph.sb([128, 2, NKT, 128], BF16)
    bvt, bkdec = Buf(), Buf()
    fw.dma("sp", vtok[:, :, :], g.VTOK[:, :].rearrange("(c p) f -> p c f", p=128), writes=[bvt], sbuf=bvt)
    for d in range(2):
        fw.dma("sp", kdec[:, d, :, :], g.KDEC[d, :, :].rearrange("(c p) f -> p c f", p=128), writes=[bkdec], sbuf=bkdec, nodeps=True)
    maskT = ph.sb([128, 2, 4, 128], F32)
    fq = ph.sb([32, 2, 4, 128], F32)
    g128 = ph.sb([32, 8], F32)
    bmask, bfq, bg128 = Buf(), Buf(), Buf()
    s = 32.0 ** -0.5
    for d in range(2):
        dm = g.c_dmat if d == 0 else g.c_dmatn
        mk = g.c_mge if d == 0 else g.c_mle
        qi = g.c_ip1 if d == 0 else g.c_128mi
        for hd in range(4):
            j = d * 4 + hd
            fw.op("act", lambda h: h.activation(out=maskT[:, d, hd, :], in_=dm, func=AF.Exp, scale=g.lg[:, l, j:j + 1]),
                  reads=[g.blg, g.bconst], writes=[bmask])
            fw.op("dve", lambda h: h.scalar_tensor_tensor(out=maskT[:, d, hd, :], in0=maskT[:, d, hd, :], scalar=s, in1=mk,
                                                          op0=ALU.mult, op1=ALU.mult), reads=[bmask, g.bconst], writes=[bmask])
            fw.op("act", lambda h: h.activation(out=fq[:, d, hd, :], in_=qi[0:32, :], func=AF.Exp, scale=g.lg[0:32, l, j:j + 1]),
                  reads=[g.blg, g.bconst], writes=[bfq])
    fw.op("act", lambda h: h.activation(out=g128[:, :], in_=g.lg[0:32, l, :], func=AF.Exp, scale=128.0), reads=[g.blg], writes=[bg128])
    S32 = ph.sb([32, 2, 4, 64], F32)
    Sbf = ph.sb([32, 2, 2, 4, 64], BF16)
    bS = [Buf(), Buf()]
    bSbf = [[Buf(), Buf()], [Buf(), Buf()]]
    for d in range(2):
        fw.op("dve", lambda h: h.memset(S32[:, d, :, :], 0.0), writes=[bS[d]])
        fw.op("dve", lambda h: h.memset(Sbf[:, d, 0, :, :], 0.0), writes=[bSbf[d][0]])
    yf = ph.sb([64, 4, T], BF16)
    byf = [Buf() for _ in range(NKT)]
    sc_ps = ph.ring(2, [128, 4, 128], F32, psum=True)
    o_ps = ph.ring(2, [64, 4, 128], F32, psum=True)
    su_ps = ph.ring(2, [32, 4, 64], F32, psum=True)
    ss_ps = ph.ring(2, [64, 512], F32, psum=True)
    sc = ph.ring(2, [128, 4, 128], BF16)
    qd = ph.ring(2, [32, 4, 128], BF16)
    osb = ph.ring(6, [64, 4, 128], F32)
    sqo = ph.ring(2, [64, 4, 128], BF16)
    lnv = ph.ring(2, [64, 512], F32)
    rst = ph.ring(4, [64, 512], F32)
    on = ph.ring(2, [64, 4, 128], F32)
    sg = ph.ring(6, [64, 4, 128], BF16)
    yb = ph.ring(2, [64, 4, 128], F32)
    yo = ph.ring(2, [64, 4, 128], BF16)
    order_f = list(range(NKT))
    order_b = [1, 0] + list(range(NKT - 1, 1, -1))
    visited = set()

    def stage_a(k, d, c):
        c0 = c * 128
        need_out = ctx_out or c >= 2
        st = dict(d=d, c=c, need_out=need_out)
        if need_out:
            sg_t, bsg = sg.next()
            fw.dma("sp", sg_t[:, :, :], g.SG[d, :, :, c0:c0 + 128].rearrange("h p t -> p h t"), writes=[bsg], sbuf=bsg)
            st["sg"] = (sg_t, bsg)
            sct, bsc = sc_ps.next()
            for hd in range(4):
                fw.op("pe", lambda h: h.matmul(sct[:, hd, :], kT[hd][:, c0:c0 + 128], qT[hd][:, c0:c0 + 128], start=True, stop=True),
                      reads=[bk[hd], bq[hd]], writes=[bsc], inc=(hd == 3))
            sc_t, bscs = sc.next()
            fw.op("dve", lambda h: h.tensor_tensor(out=sc_t[:, :, :], in0=sct[:, :, :], in1=maskT[:, d, :, :], op=ALU.mult),
                  reads=[bsc, bmask], writes=[bscs])
            qd_t, bqd = qd.next()
            for hd in range(4):
                fw.op("pool", lambda h: h.tensor_tensor(out=qd_t[:, hd, :], in0=qT[hd][:, c0:c0 + 128], in1=fq[:, d, hd, :], op=ALU.mult),
                      reads=[bq[hd], bfq], writes=[bqd])
        sut, bsu = su_ps.next()
        for hd in range(4):
            fw.op("pe", lambda h: h.matmul(sut[:, hd, :], kdec[:, d, c, hd * 32:(hd + 1) * 32], vtok[:, c, hd * 64:(hd + 1) * 64], start=True, stop=True),
                  reads=[bkdec, bvt], writes=[bsu], inc=(hd == 3))
        if need_out:
            ot, bo = o_ps.next()
            for hd in range(4):
                fw.op("pe", lambda h: h.matmul(ot[:, hd, :], vtok[:, c, hd * 64:(hd + 1) * 64], sc_t[:, hd, :], start=True, stop=False),
                      reads=[bvt, bscs], writes=[bo], inc=False)
                fw.op("pe", lambda h: h.matmul(ot[:, hd, :], Sbf[:, d, k % 2, hd, :], qd_t[:, hd, :], start=False, stop=True),
                      reads=[bSbf[d][k % 2], bqd], writes=[bo], inc=(hd == 3))
            os_t, bos = osb.next()
            fw.op("act", lambda h: h.activation(out=os_t[:, :, :], in_=ot[:, :, :], func=AF.Copy), reads=[bo], writes=[bos])
            st["o"] = (os_t, bos)
        for hd in range(4):
            fw.op("dve", lambda h: h.scalar_tensor_tensor(out=S32[:, d, hd, :], in0=S32[:, d, hd, :], scalar=g128[:, d * 4 + hd:d * 4 + hd + 1],
                                                          in1=sut[:, hd, :], op0=ALU.mult, op1=ALU.add),
                  reads=[bS[d], bg128, bsu], writes=[bS[d]])
        fw.op("dve", lambda h: h.tensor_copy(out=Sbf[:, d, (k + 1) % 2, :, :], in_=S32[:, d, :, :]), reads=[bS[d]], writes=[bSbf[d][(k + 1) % 2]])
        return st

    def stage_b(st):
        if not st["need_out"]:
            return
        os_t, bos = st["o"]
        sq_t, bsq = sqo.next()
        fw.op("act", lambda h: h.activation(out=sq_t[:, :, :], in_=os_t[:, :, :], func=AF.Square), reads=[bos], writes=[bsq])
        sst, bss = ss_ps.next()
        fw.op("pe", lambda h: h.matmul(sst[:, :], g.ones_bf[0:64, 0:64], sq_t[:, :, :].rearrange("p a b -> p (a b)"), start=True, stop=True),
              reads=[bsq, g.bconst], writes=[bss])
        ln_t, bln = lnv.next()
        rs_t, brs = rst.next()
        fw.op("act", lambda h: h.activation(out=ln_t[:, :], in_=sst[:, :], func=AF.Ln, bias=g.epsb[0:64, :], scale=1.0 / 64), reads=[bss, g.bconst], writes=[bln])
        fw.op("act", lambda h: h.activation(out=rs_t[:, :], in_=ln_t[:, :], func=AF.Exp, scale=-0.5), reads=[bln], writes=[brs])
        st["rs"] = (rs_t, brs)

    def stage_c(st):
        if not st["need_out"]:
            return
        d, c = st["d"], st["c"]
        c0 = c * 128
        os_t, bos = st["o"]
        rs_t, brs = st["rs"]
        sg_t, bsg = st["sg"]
        on_t, bon = on.next()
        fw.op("dve", lambda h: h.tensor_tensor(out=on_t[:, :, :], in0=os_t[:, :, :], in1=rs_t[:, :].rearrange("p (a b) -> p a b", a=4), op=ALU.mult),
              reads=[bos, brs], writes=[bon])
        if c not in visited:
            visited.add(c)
            fw.op("pool", lambda h: h.tensor_tensor(out=yf[:, :, c0:c0 + 128], in0=on_t[:, :, :], in1=sg_t[:, :, :], op=ALU.mult),
                  reads=[bon, bsg], writes=[byf[c]])
        else:
            yb_t, byb = yb.next()
            yo_t, byo = yo.next()
            fw.op("pool", lambda h: h.tensor_tensor(out=yb_t[:, :, :], in0=on_t[:, :, :], in1=sg_t[:, :, :], op=ALU.mult),
                  reads=[bon, bsg], writes=[byb])
            fw.op("pool", lambda h: h.tensor_tensor(out=yo_t[:, :, :], in0=yb_t[:, :, :], in1=yf[:, :, c0:c0 + 128], op=ALU.add),
                  reads=[byb, byf[c]], writes=[byo])
            fw.dma("sp", g.YC[768:1024, c0:c0 + 128].rearrange("(h p) t -> p h t", p=64), yo_t[:, :, :], reads=[byo], sbuf=byo)

    hist = {}
    for k in range(NKT + 2):
        if k < NKT:
            hist[k] = [stage_a(k, 0, order_f[k]), stage_a(k, 1, order_b[k])]
        if 0 <= k - 1 < NKT:
            for st in hist[k - 1]:
                stage_b(st)
        if 0 <= k - 2 < NKT:
            for st in hist.pop(k - 2):
                stage_c(st)
    ph.close()


NCONST = 6 * 128 + 2


def build(n_layers=DEPTH, taps=(), stop_after=None, layer_list=None):
    nc = bass.Bass("TRN2", target_bir_lowering=False)
    g = G()

    def din(name, shape, dt=F32):
        return nc.dram_tensor(name, list(shape), dt, kind="ExternalInput").ap()

    def dscr(name, shape, dt):
        kind = "ExternalOutput" if name in taps else "Internal"
        return nc.dram_tensor(name, list(shape), dt, kind=kind).ap()

    g.xin = din("xin", [D, T])
    g.cc = din("cc", [128, 8, 2])
    g.w_mod = din("w_mod", [DEPTH, D, 6 * D])
    g.b_modT = din("b_modT", [128, DEPTH, 48])
    g.w_in = din("w_in", [DEPTH, D, NU])
    g.w_out = din("w_out", [DEPTH, D, D])
    g.conv_dwT = din("conv_dwT", [DEPTH, 128, 2, 31])
    g.conv_pw = din("conv_pw", [DEPTH, 256, 256])
    g.uqA = din("uqA", [DEPTH, 256, 384])
    g.uqB = din("uqB", [DEPTH, 256, 384])
    g.ukv = din("ukv", [DEPTH, 128, 512])
    g.w1 = din("w1", [DEPTH, D, 4 * D])
    g.w2 = din("w2", [DEPTH, 4 * D, D])
    g.vecs_d = din("vecs", [128, DEPTH, NV])
    g.fing_d = din("fing", [128, 8])
    g.cosg = din("cosg", [128, T])
    g.sing = din("sing", [128, T])
    g.cosm = din("cosm", [32, T])
    g.sinm = din("sinm", [32, T])
    g.ident_d = din("ident", [128, 128])
    g.blk_d = din("blk", [128, 128])
    g.consts_d = din("consts", [128, NCONST])
    g.out = nc.dram_tensor("out", [D, L], F32, kind="ExternalOutput").ap()
    g.XR = dscr("XR", [D, T], F32)
    g.UT = dscr("UT", [NU, T], BF16)
    g.YC = dscr("YC", [D, T], BF16)
    g.H2 = dscr("H2", [D, T], BF16)
    g.QTm = dscr("QTm", [4, 96, T], BF16)
    g.KTm = dscr("KTm", [4, 96, T], BF16)
    g.VAm = dscr("VAm", [T, 4 * 128], BF16)
    g.QTg = dscr("QTg", [4, 64, T], BF16)
    g.KTg = dscr("KTg", [2, 64, T], BF16)
    g.VAg = dscr("VAg", [T, 2 * 128], BF16)
    g.KDEC = dscr("KDEC", [2, T, 128], BF16)
    g.VTOK = dscr("VTOK", [T, 256], BF16)
    g.SG = dscr("SG", [2, 4, 64, T], BF16)
    g.MODT = dscr("MODT", [128, DEPTH * 2 * 48], F32)

    fw = FW(nc)
    res = ExitStack()

    def rsb(name, shape, dt):
        return res.enter_context(nc.sbuf_tensor(name, list(shape), dt))

    g.modT = rsb("modT", [128, DEPTH, 2, 48], F32)
    g.vecs = rsb("vecs_sb", [128, DEPTH, NV], F32)
    g.fing = rsb("fing_sb", [128, 8], F32)
    g.consts = rsb("consts_sb", [128, NCONST], F32)
    g.ident_bf = rsb("ident_bf", [128, 128], BF16)
    g.blk_bf = rsb("blk_bf", [128, 128], BF16)
    g.ones_bf = rsb("ones_bf", [128, 128], BF16)
    g.ones_f = rsb("ones_f", [128, 128], F32)
    g.epsb = rsb("epsb", [128, 1], F32)
    g.oneb = rsb("oneb", [128, 1], F32)
    g.lg = rsb("lg_sb", [128, DEPTH, 8], F32)
    g.bm = rsb("bm_sb", [128, DEPTH, 48], F32)
    g.sl_bf = rsb("sl_bf", [128, 8, 2], BF16)
    g.ident_f = rsb("ident_f", [128, 128], F32)
    g.bmod, g.bconst, g.blg = Buf(), Buf(), Buf()
    fw.persistent = [g.bmod, g.bconst, g.blg]
    cs = g.consts
    g.c_dmat = cs[:, 0:128]
    g.c_dmatn = cs[:, 128:256]
    g.c_mge = cs[:, 256:384]
    g.c_mle = cs[:, 384:512]
    g.c_ip1 = cs[:, 512:640]
    g.c_128mi = cs[:, 640:768]
    g.c_rev = cs[:, 768:769]
    g.c_pidx = cs[:, 769:770]

    ph = Phase(fw)
    b1, b2, b3, b4, b5 = Buf(), Buf(), Buf(), Buf(), Buf()
    fw.dma("sp", g.vecs[:, :, :], g.vecs_d[:, :, :], writes=[b1], sbuf=b1)
    fw.dma("sp", g.fing[:, :], g.fing_d[:, :], writes=[b2], sbuf=b2)
    fw.dma("sp", g.consts[:, :], g.consts_d[:, :], writes=[b3], sbuf=b3)
    fw.dma("pool", g.ident_bf[:, :], g.ident_d[:, :], writes=[b4], sbuf=b4)
    fw.dma("pool", g.blk_bf[:, :], g.blk_d[:, :], writes=[b5], sbuf=b5)
    fw.op("dve", lambda h: h.memset(g.ones_bf[:, :], 1.0))
    fw.op("dve", lambda h: h.memset(g.ones_f[:, :], 1.0))
    fw.op("dve", lambda h: h.memset(g.epsb[:, :], EPS))
    fw.op("dve", lambda h: h.memset(g.oneb[:, :], 1.0))
    b6, b7, b8 = Buf(), Buf(), Buf()
    cc_sb = ph.sb([128, 8, 2], F32)
    fw.dma("sp", cc_sb[:], g.cc[:, :, :], writes=[b6], sbuf=b6)
    fw.dma("sp", g.bm[:, :, :], g.b_modT[:, :, :], writes=[b7], sbuf=b7)
    fw.dma("sp", g.ident_f[:, :], g.ident_d[:, :], writes=[b8], sbuf=b8)
    fw.op("act", lambda h: h.activation(out=g.sl_bf[:], in_=cc_sb[:], func=AF.Silu), reads=[b6])
    ph.close()

    def done():
        res.close()
        fw.es.close()
        return nc

    ph = Phase(fw)
    first_l = layer_list[0] if layer_list is not None else 0
    emit_mod(fw, g, ph, first_l)
    ph.close()
    if "MODT" in taps:
        ph = Phase(fw)
        bt = Buf()
        fw.dma("sp", g.MODT[:, :], g.modT[:, :, :, :].rearrange("p a b c -> p (a b c)"), sbuf=bt)
        ph.close()
    if stop_after == "mod":
        return done()
    for l in (layer_list if layer_list is not None else range(n_layers)):
        last = (l == DEPTH - 1)
        ctx_out = not last
        xsrc = g.xin if l == 0 else g.XR
        phase_inproj(fw, g, l, xsrc)
        if stop_after == "inproj":
            return done()
        phase_prep(fw, g, l, ctx_out)
        if stop_after == "conv":
            return done()
        phase_attn(fw, g, g.QTm, g.KTm, g.VAm, 4, 4, 96, 96.0 ** -0.5, 256, ctx_out)
        if stop_after == "mla":
            return done()
        phase_attn(fw, g, g.QTg, g.KTg, g.VAg, 4, 2, 64, 64.0 ** -0.5, 512, ctx_out)
        if stop_after == "gqa":
            return done()
        phase_ret_prep(fw, g, l, mod_next=(l + 1 if (layer_list is None and l + 1 < n_layers) else None))
        phase_ret_scan(fw, g, l, ctx_out)
        if stop_after == "ret":
            return done()
        phase_outproj(fw, g, l, xsrc, last)
        if stop_after == "outproj":
            return done()
        phase_mlp(fw, g, l, last)
    phase_final(fw, g)
    return done()


def _rope_tables(ndim):
    half2 = ndim // 2
    hh = half2 // 2
    t = np.arange(L)
    row = (t // 64).astype(np.float32)
    col = (t % 64).astype(np.float32)
    freqs = (np.float32(10000.0) ** (-np.arange(hh, dtype=np.float32) / np.float32(hh))).astype(np.float32)
    cos = np.ones((ndim, T), np.float32)
    sin = np.zeros((ndim, T), np.float32)
    partner = np.zeros(ndim, np.int64)
    for d in range(ndim):
        blk = d // half2
        r = d % half2
        first = r < hh
        i = r % hh
        pos = row if blk == 0 else col
        ang = (pos * freqs[i]).astype(np.float32)
        cos[d, CT:] = np.cos(ang).astype(np.float32)
        sin[d, CT:] = (-np.sin(ang) if first else np.sin(ang)).astype(np.float32)
        partner[d] = d + hh if first else d - hh
    return cos, sin, partner


def _prep_shared(inp):
    f = np.float32
    cos_g, sin_g, part_g = _rope_tables(64)
    cos_m, sin_m, part_m = _rope_tables(32)
    sh = {}
    sh["cosg"] = np.ascontiguousarray(np.concatenate([cos_g, cos_g], 0))
    sh["sing"] = np.ascontiguousarray(np.concatenate([sin_g, sin_g], 0))
    sh["cosm"] = cos_m
    sh["sinm"] = sin_m
    w_in = inp["w_in"]
    o_av, o_ag, o_cq, o_ckv, o_kpe, o_q, o_k, o_v, o_rq, o_rk, o_rv, o_gf, o_gb = (
        0, 256, 512, 768, 896, 928, 1184, 1312, 1440, 1568, 1696, 1952, 2208)
    cols = []
    cols += list(range(o_av, o_av + 256)) + list(range(o_ag, o_ag + 256)) + list(range(o_cq, o_cq + 256)) + list(range(o_ckv, o_ckv + 128))
    qhead_order = [0, 2, 1, 3]
    qa = [o_q + h * 64 + d for h in qhead_order for d in range(64)]
    qb = [o_q + h * 64 + int(part_g[d]) for h in qhead_order for d in range(64)]
    ka = [o_k + h * 64 + d for h in range(2) for d in range(64)]
    kb = [o_k + h * 64 + int(part_g[d]) for h in range(2) for d in range(64)]
    cols += qa + qb + ka + kb
    cols += list(range(o_v, o_v + 128)) + list(range(o_rq, o_rq + 128)) + list(range(o_rk, o_rk + 128))
    cols += list(range(o_rv, o_rv + 256)) + list(range(o_gf, o_gf + 256)) + list(range(o_gb, o_gb + 256))
    cols += [o_kpe + d for d in range(32)] + [o_kpe + int(part_m[d]) for d in range(32)]
    assert len(cols) == NU
    sh["w_in"] = np.ascontiguousarray(w_in[:, :, np.array(cols)])
    uq = inp["mla_uq"]
    ub_cols = []
    for h in range(4):
        ub_cols += [h * 96 + c for c in range(64)] + [h * 96 + 64 + int(part_m[d]) for d in range(32)]
    sh["uqA"] = np.ascontiguousarray(uq)
    sh["uqB"] = np.ascontiguousarray(uq[:, :, np.array(ub_cols)])
    sh["ukv"] = np.ascontiguousarray(inp["mla_ukv"])
    sh["w_mod"] = np.ascontiguousarray(inp["w_mod"])
    sh["b_modT"] = np.ascontiguousarray(inp["b_mod"].reshape(DEPTH, 48, 128).transpose(2, 0, 1))
    sh["w_out"] = np.ascontiguousarray(inp["w_out"])
    sh["conv_dwT"] = np.ascontiguousarray(inp["conv_dw"].reshape(DEPTH, 31, 2, 128).transpose(0, 3, 2, 1))
    sh["conv_pw"] = np.ascontiguousarray(inp["conv_pw"])
    sh["w1"] = np.ascontiguousarray(inp["mlp_w1"])
    sh["w2"] = np.ascontiguousarray(inp["mlp_w2"])
    vecs = np.zeros((128, DEPTH, NV), f)

    def pc(v):
        return v.reshape(DEPTH, 2, 128).transpose(2, 0, 1)

    vecs[:, :, 0:2] = pc(inp["conv_b"])
    vecs[:, :, 2:4] = pc(inp["conv_ln_g"])
    vecs[:, :, 4:6] = pc(inp["conv_ln_b"])
    vecs[:, :, 6:8] = pc(inp["mla_q_g"])
    vecs[:, :, 8] = inp["mla_kv_g"].T
    gq = inp["gqa_q_g"]
    gk = inp["gqa_k_g"]
    vecs[:, :, 9] = np.concatenate([gq, gq], 1).T
    vecs[:, :, 10] = np.concatenate([gq[:, part_g], gq[:, part_g]], 1).T
    vecs[:, :, 11] = np.concatenate([gk, gk], 1).T
    vecs[:, :, 12] = np.concatenate([gk[:, part_g], gk[:, part_g]], 1).T
    rn = inp["ret_norm_g"]
    rd = inp["ret_decay"]
    for d in range(2):
        for h in range(4):
            vecs[:, :, 13 + d * 4 + h] = np.concatenate([rn[:, d, h, :], rn[:, d, h, :]], 1).T
            vecs[:, :, 21 + d * 4 + h] = np.broadcast_to(rd[:, d, h][None, :], (128, DEPTH))
    sh["vecs"] = vecs
    sh["fing"] = np.ascontiguousarray(inp["final_g"].reshape(8, 128).T)
    sh["ident"] = np.eye(128, dtype=f)
    blk = np.zeros((128, 128), f)
    blk[:64, :64] = 1
    blk[64:, 64:] = 1
    sh["blk"] = blk
    p = np.arange(128)[:, None].astype(f)
    i = np.arange(128)[None, :].astype(f)
    cst = np.zeros((128, NCONST), f)
    cst[:, 0:128] = np.maximum(i - p, 0)
    cst[:, 128:256] = np.maximum(p - i, 0)
    cst[:, 256:384] = (i >= p)
    cst[:, 384:512] = (i <= p)
    cst[:, 512:640] = i + 1
    cst[:, 640:768] = 128 - i
    cst[:, 768] = 127 - p[:, 0]
    cst[:, 769] = p[:, 0]
    sh["consts"] = cst
    return sh


def _prep_core(inp, b):
    xin = np.ascontiguousarray(np.concatenate([inp["ctx"][b], inp["x"][b]], 0).T)
    cc = np.stack([inp["c"][b].reshape(8, 128).T, inp["c_ctx"].reshape(8, 128).T], -1)
    return {"xin": xin, "cc": np.ascontiguousarray(cc.astype(np.float32))}


_NC_CACHE = {}


def kernel(**inputs):
    inp = {k: np.asarray(v, dtype=np.float32) for k, v in inputs.items()}
    sh = _prep_shared(inp)
    if "nc" not in _NC_CACHE:
        _NC_CACHE["nc"] = build()
    nc = _NC_CACHE["nc"]
    in_maps = []
    for b in range(8):
        m = dict(sh)
        m.update(_prep_core(inp, b))
        in_maps.append(m)
    res = run_bass_kernel_spmd(nc, in_maps, core_ids=list(range(8)))
    out = np.stack([np.ascontiguousarray(res.results[b]["out"].T) for b in range(8)], 0)
    return out.astype(np.float32)
```

```python
import numpy as np
import concourse.bass as bass
import concourse.mybir as mybir
from concourse.bass_utils import run_bass_kernel_spmd
from contextlib import ExitStack

F32 = mybir.dt.float32
BF16 = mybir.dt.bfloat16
AF = mybir.ActivationFunctionType
ALU = mybir.AluOpType

D = 1024
L = 4096
CT = 256
T = L + CT
DEPTH = 4
NU = 2880
EPS = 1e-6
TILES = [(0, 256)] + [(256 + 512 * i, 512) for i in range(8)]
NKT = T // 128
NV = 29


ATTACH_WAITS = True


class Buf:
    __slots__ = ("w", "r", "sem", "cnt")

    def __init__(self):
        self.w = []
        self.r = []
        self.sem = None
        self.cnt = 0


class FW:
    def __init__(self, nc):
        self.nc = nc
        self.es = ExitStack()
        self.eng = {}
        for name, h in (("pe", nc.tensor), ("act", nc.scalar), ("dve", nc.vector), ("pool", nc.gpsimd), ("sp", nc.sync)):
            sem = self.es.enter_context(nc.semaphore(name + "_ms"))
            self.eng[name] = dict(h=h, sem=sem, cnt=0, seen={}, name=name)
        self.free_sems = []
        self.phase_sems = []
        self.nsem = 0
        self.outstanding = []
        self.uid = 0
        self.pool_sems = [dict(sem=self.es.enter_context(nc.semaphore("psem%d" % i)), cnt=0) for i in range(8)]
        self.pool_rr = 0

    def get_sem(self):
        if self.free_sems:
            s = self.free_sems.pop()
        else:
            self.nsem += 1
            s = self.es.enter_context(self.nc.semaphore("dsem%d" % self.nsem))
        self.phase_sems.append(s)
        return s

    def _wait(self, e, ev):
        if ev is None:
            return
        sem, val = ev
        key = id(sem)
        if e["seen"].get(key, 0) >= val:
            return
        if sem is e["sem"] and e["name"] == "pe":
            return
        e["h"].wait_ge(sem, val)
        e["seen"][key] = val

    def _needed(self, e, reads, writes):
        need = {}
        for b in reads:
            for (sem, val) in b.w:
                if need.get(id(sem), (None, 0))[1] < val:
                    need[id(sem)] = (sem, val)
        for b in writes:
            for (sem, val) in list(b.w) + list(b.r):
                if need.get(id(sem), (None, 0))[1] < val:
                    need[id(sem)] = (sem, val)
        out = []
        for (sem, val) in need.values():
            if e["seen"].get(id(sem), 0) >= val:
                continue
            if sem is e["sem"] and e["name"] == "pe":
                continue
            out.append((sem, val))
        return out

    def _deps(self, e, reads, writes):
        for ev in self._needed(e, reads, writes):
            self._wait(e, ev)

    def op(self, engname, fn, reads=(), writes=(), inc=True):
        e = self.eng[engname]
        need = self._needed(e, reads, writes)
        attach = None
        if ATTACH_WAITS and need:
            attach = need.pop()
        for ev in need:
            self._wait(e, ev)
        inst = fn(e["h"])
        if attach is not None:
            inst._wait_ge(attach[0], attach[1])
            e["seen"][id(attach[0])] = attach[1]
        if inc:
            e["cnt"] += 1
            inst.then_inc(e["sem"], 1)
            ev = (e["sem"], e["cnt"])
        else:
            ev = (e["sem"], e["cnt"] + 1)
        for b in reads:
            b.r.append(ev)
        for b in writes:
            b.w = [ev]
            b.r = []
        return ev

    def dma(self, qname, out, in_, reads=(), writes=(), sbuf=None, nodeps=False, **kw):
        e = self.eng[qname]
        if not nodeps:
            self._deps(e, reads, writes)
        if qname == "pool":
            ps = self.pool_sems[self.pool_rr % len(self.pool_sems)]
            self.pool_rr += 1
            if ps["cnt"] > 0:
                self._wait(e, (ps["sem"], ps["cnt"]))
            ps["cnt"] += 16
            e["h"].dma_start(out=out, in_=in_, **kw).then_inc(ps["sem"], 16)
            ev = (ps["sem"], ps["cnt"])
        else:
            if sbuf.sem is None:
                sbuf.sem = self.get_sem()
            sbuf.cnt += 16
            e["h"].dma_start(out=out, in_=in_, **kw).then_inc(sbuf.sem, 16)
            ev = (sbuf.sem, sbuf.cnt)
        for b in reads:
            b.r.append(ev)
        for b in writes:
            if nodeps:
                b.w = [x for x in b.w if x[0] is not ev[0]] + [ev]
            else:
                b.w = [ev]
                b.r = []
        self.outstanding.append(ev)
        return ev

    def phase_end(self):
        e = self.eng["sp"]
        mx = {}
        for (sem, val) in self.outstanding:
            if mx.get(id(sem), (None, 0))[1] < val:
                mx[id(sem)] = (sem, val)
        for ev in mx.values():
            self._wait(e, ev)
        self.outstanding = []
        self.nc.all_engine_barrier()
        for s in self.phase_sems:
            self.nc.sync.sem_clear(s)
            for e2 in self.eng.values():
                e2["seen"].pop(id(s), None)
            self.free_sems.append(s)
        self.phase_sems = []
        for b in getattr(self, "persistent", []):
            b.w = []
            b.r = []
        for e2 in self.eng.values():
            self.nc.sync.sem_clear(e2["sem"])
            e2["cnt"] = 0
            e2["seen"] = {}
        self.nc.all_engine_barrier()


class Phase:
    def __init__(self, fw):
        self.fw = fw
        self.nc = fw.nc
        self.es = ExitStack()

    def sb(self, shape, dt):
        self.fw.uid += 1
        return self.es.enter_context(self.nc.sbuf_tensor("sb%d" % self.fw.uid, list(shape), dt))

    def ps(self, shape=(128, 512), dt=F32):
        self.fw.uid += 1
        return self.es.enter_context(self.nc.psum_tensor("ps%d" % self.fw.uid, list(shape), dt))

    def ring(self, n, shape, dt, psum=False):
        return Ring([(self.ps(shape, dt) if psum else self.sb(shape, dt), Buf()) for _ in range(n)])

    def close(self):
        self.fw.phase_end()
        self.es.close()


class Ring:
    def __init__(self, items):
        self.items = items
        self.i = -1

    def next(self):
        self.i = (self.i + 1) % len(self.items)
        return self.items[self.i]


class G:
    pass


def cast_load(fw, dst, src, buf, ncols):
    if ncols <= 1024:
        fw.dma("pool", dst, src, writes=[buf], sbuf=buf, nodeps=True)
        return
    a = -(-ncols // 1024)
    while ncols % a:
        a += 1
    fw.dma("pool", dst.rearrange("p (a b) -> p a b", a=a), src.rearrange("p (a b) -> p a b", a=a),
           writes=[buf], sbuf=buf, nodeps=True)


def cast_load3(fw, dst, src, buf):
    fw.dma("pool", dst, src.rearrange("(kc p) n -> p kc n", p=128), writes=[buf], sbuf=buf, nodeps=True)


def emit_mod(fw, g, ph, l):
    wm = ph.ring(2, [128, 8, 1536], BF16)
    mp = ph.ring(2, [2, 512], F32, psum=True)
    tp = ph.ps([128, 48, 2], F32)
    btp = Buf()
    mrow = ph.sb([2, 6 * D], F32)
    bmrow = [Buf() for _ in range(12)]
    for cg in range(4):
        wt, bw = wm.next()
        fw.dma("pool", wt[:, :, :], g.w_mod[l, :, cg * 1536:(cg + 1) * 1536].rearrange("(kc p) n -> p kc n", p=128),
               writes=[bw], sbuf=bw)
        for q in range(3):
            ng = cg * 3 + q
            mpt, bmp = mp.next()
            for kc in range(8):
                fw.op("pe", lambda h: h.matmul(mpt[:, :], g.sl_bf[:, kc, :], wt[:, kc, q * 512:(q + 1) * 512],
                                               start=(kc == 0), stop=(kc == 7)),
                      reads=[bw, g.bconst], writes=[bmp], inc=(kc == 7))
            fw.op("dve", lambda h: h.tensor_copy(out=mrow[:, ng * 512:(ng + 1) * 512], in_=mpt[:, :]), reads=[bmp], writes=[bmrow[ng]])
    for j in range(48):
        fw.op("pe", lambda h: h.matmul(tp[:, j, :], mrow[:, j * 128:(j + 1) * 128], g.ident_f[0:2, 0:2], start=True, stop=True),
              reads=[bmrow[j // 4], g.bconst], writes=[btp], inc=(j == 47))
    for cond in range(2):
        fw.op("dve", lambda h: h.tensor_tensor(out=g.modT[:, l, cond, :], in0=tp[:, :, cond], in1=g.bm[:, l, :],
                                               op=ALU.add), reads=[btp, g.bconst], writes=[g.bmod])
    for j0 in (8, 32):
        fw.op("dve", lambda h: h.tensor_scalar(out=g.modT[:, l, :, j0:j0 + 8], in0=g.modT[:, l, :, j0:j0 + 8],
                                               scalar1=1.0, scalar2=None, op0=ALU.add),
              reads=[g.bmod], writes=[g.bmod])


def rstd_from_sq(fw, g, sq_chunks, bsq, lhs_ones, ssp, bssp, sd, bsd, rstd, brstd, W, nfeat, P=128):
    n = len(sq_chunks)
    for i, ap in enumerate(sq_chunks):
        fw.op("pe", lambda h: h.matmul(ssp[:P, :W], lhs_ones, ap, start=(i == 0), stop=(i == n - 1)),
              reads=[bsq, g.bconst], writes=[bssp], inc=(i == n - 1))
    fw.op("act", lambda h: h.activation(out=sd[:P, :W], in_=ssp[:P, :W], func=AF.Ln, bias=g.epsb[:P, :], scale=1.0 / nfeat),
          reads=[bssp, g.bconst], writes=[bsd])
    fw.op("act", lambda h: h.activation(out=rstd[:P, :W], in_=sd[:P, :W], func=AF.Exp, scale=-0.5), reads=[bsd], writes=[brstd])


def norm_modulate(fw, g, ph_bufs, xt, bxt, hh, bh, W, sc_ap, sh_ap):
    sq, bsq, ssp, bssp, sd, bsd, rstd, brstd, tt, btt = ph_bufs
    fw.op("act", lambda h: h.activation(out=sq[:, :, :W], in_=xt[:, :, :W], func=AF.Square), reads=[bxt], writes=[bsq])
    rstd_from_sq(fw, g, [sq[:, kc, :W] for kc in range(8)], bsq, g.ones_bf[:, :], ssp, bssp, sd, bsd, rstd, brstd, W, D)
    for kc in range(8):
        fw.op("dve", lambda h: h.tensor_tensor(out=tt[:, kc, :W], in0=xt[:, kc, :W], in1=rstd[:, :W], op=ALU.mult),
              reads=[bxt, brstd], writes=[btt[kc]])
        fw.op("act", lambda h: h.activation(out=hh[:, kc, :W], in_=tt[:, kc, :W], func=AF.Identity,
                                            bias=sh_ap(kc), scale=sc_ap(kc)),
              reads=[btt[kc], g.bmod], writes=[bh[kc]])


def norm_modulate_gen(fw, g, ph_bufs, xt, bxt, hh, bh, W, sc_ap, sh_ap):
    sq, bsq, ssp, bssp, sd, bsd, rstd, brstd, tt, btt = ph_bufs
    fw.op("act", lambda h: h.activation(out=sq[:, :, :W], in_=xt[:, :, :W], func=AF.Square), reads=[bxt], writes=[bsq])
    yield
    rstd_from_sq(fw, g, [sq[:, kc, :W] for kc in range(8)], bsq, g.ones_bf[:, :], ssp, bssp, sd, bsd, rstd, brstd, W, D)
    yield
    for kc in range(8):
        fw.op("dve", lambda h: h.tensor_tensor(out=tt[:, kc, :W], in0=xt[:, kc, :W], in1=rstd[:, :W], op=ALU.mult),
              reads=[bxt, brstd], writes=[btt[kc]])
        fw.op("act", lambda h: h.activation(out=hh[:, kc, :W], in_=tt[:, kc, :W], func=AF.Identity,
                                            bias=sh_ap(kc), scale=sc_ap(kc)),
              reads=[btt[kc], g.bmod], writes=[bh[kc]])
        yield


def norm_bufs(ph):
    sq = ph.sb([128, 8, 512], BF16)
    ssp = ph.ps()
    sd = ph.sb([128, 512], F32)
    rstd = ph.sb([128, 512], F32)
    tt = ph.sb([128, 8, 512], F32)
    return (sq, Buf(), ssp, Buf(), sd, Buf(), rstd, Buf(), tt, [Buf() for _ in range(8)])


def phase_inproj(fw, g, l, xsrc):
    ph = Phase(fw)
    win = ph.sb([128, 8, NU], BF16)
    bwin = [Buf() for _ in range(3)]
    for cb in range(3):
        cast_load3(fw, win[:, :, cb * 960:(cb + 1) * 960], g.w_in[l, :, cb * 960:(cb + 1) * 960], bwin[cb])
    xt = ph.ring(2, [128, 8, 512], F32)
    nb = norm_bufs(ph)
    hh2 = [ph.sb([128, 8, 512], BF16) for _ in range(2)]
    bh2 = [[Buf() for _ in range(8)] for _ in range(2)]
    up = ph.ring(4, [128, 512], F32, psum=True)
    ut = [ph.sb([128, 23, 512], BF16) for _ in range(2)]
    but = [[Buf() for _ in range(23)] for _ in range(2)]
    loaded = {}
    NT = len(TILES)

    def load(ti):
        t0, W = TILES[ti]
        x_t, bx = xt.next()
        fw.dma("sp", x_t[:, :, :W], xsrc[:, t0:t0 + W].rearrange("(kc p) t -> p kc t", p=128), writes=[bx], sbuf=bx)
        loaded[ti] = (x_t, bx)

    def norm(ti):
        t0, W = TILES[ti]
        x_t, bx = loaded.pop(ti)
        cond = 1 if ti == 0 else 0
        return norm_modulate_gen(fw, g, nb, x_t, bx, hh2[ti % 2], bh2[ti % 2], W,
                                 lambda kc: g.modT[:, l, cond, 8 + kc:9 + kc], lambda kc: g.modT[:, l, cond, kc:kc + 1])

    def proj(ti, side=None):
        t0, W = TILES[ti]
        hh, bh = hh2[ti % 2], bh2[ti % 2]
        u_t, bu = ut[ti % 2], but[ti % 2]
        for n in range(23):
            M = 128 if n < 22 else 64
            wb = sorted(set([(n * 128) // 960, (n * 128 + M - 1) // 960]))
            pt, bp = up.next()
            for kc in range(8):
                fw.op("pe", lambda h: h.matmul(pt[:M, :W], win[:, kc, n * 128:n * 128 + M], hh[:, kc, :W],
                                               start=(kc == 0), stop=(kc == 7)),
                      reads=[bwin[i] for i in wb] + [bh[kc]], writes=[bp], inc=(kc == 7))
            if n % 2 == 0:
                fw.op("act", lambda h: h.activation(out=u_t[:M, n, :W], in_=pt[:M, :W], func=AF.Copy), reads=[bp], writes=[bu[n]])
            else:
                fw.op("dve", lambda h: h.tensor_copy(out=u_t[:M, n, :W], in_=pt[:M, :W]), reads=[bp], writes=[bu[n]])
            if side is not None and n >= 2:
                next(side, None)
        if side is not None:
            for _ in side:
                pass
        fw.dma("sp", g.UT[0:2816, t0:t0 + W].rearrange("(c p) t -> p c t", p=128), u_t[:, 0:22, :W], reads=bu[0:22], sbuf=bu[0])
        fw.dma("sp", g.UT[2816:2880, t0:t0 + W], u_t[:64, 22, :W], reads=[bu[22]], sbuf=bu[22])

    load(0)
    load(1)
    for _ in norm(0):
        pass
    for ti in range(NT):
        if ti + 2 < NT:
            load(ti + 2)
        proj(ti, norm(ti + 1) if ti + 1 < NT else None)
    ph.close()


def phase_outproj(fw, g, l, xsrc, last):
    ph = Phase(fw)
    wo = ph.sb([128, 8, D], BF16)
    bwo2 = [Buf(), Buf()]
    for cb in range(2):
        cast_load3(fw, wo[:, :, cb * 512:(cb + 1) * 512], g.w_out[l, :, cb * 512:(cb + 1) * 512], bwo2[cb])
    xt = ph.ring(2, [128, 8, 512], F32)
    yc = ph.ring(2, [128, 8, 512], BF16)
    xn = [ph.sb([128, 8, 512], F32) for _ in range(2)]
    bxn = [Buf() for _ in range(2)]
    nb = norm_bufs(ph)
    hh = [ph.sb([128, 8, 512], BF16) for _ in range(2)]
    bhh = [[Buf() for _ in range(8)] for _ in range(2)]
    op = ph.ring(3, [128, 512], F32, psum=True)
    tiles = list(enumerate(TILES))
    if last:
        tiles = tiles[1:]
    loaded = {}

    def load(ti):
        t0, W = TILES[ti]
        x_t, bx = xt.next()
        y_t, by = yc.next()
        fw.dma("sp", x_t[:, :, :W], xsrc[:, t0:t0 + W].rearrange("(kc p) t -> p kc t", p=128), writes=[bx], sbuf=bx)
        fw.dma("sp", y_t[:, :, :W], g.YC[:, t0:t0 + W].rearrange("(kc p) t -> p kc t", p=128), writes=[by], sbuf=by)
        loaded[ti] = (x_t, bx, y_t, by)

    def norm_side(i, ti, t0, W, cond, xo, bxo):
        h_t, bh = hh[i % 2], bhh[i % 2]
        gen = norm_modulate_gen(fw, g, nb, xo, bxo, h_t, bh, W,
                                lambda kc: g.modT[:, l, cond, 32 + kc:33 + kc], lambda kc: g.modT[:, l, cond, 24 + kc:25 + kc])

        def fin():
            for _ in gen:
                pass
            fw.dma("sp", g.H2[:, t0:t0 + W].rearrange("(kc p) t -> p kc t", p=128), h_t[:, :, :W], reads=bh, sbuf=bh[0])
        return gen, fin

    load(tiles[0][0])
    side = None
    for i, (ti, (t0, W)) in enumerate(tiles):
        if i + 1 < len(tiles):
            load(tiles[i + 1][0])
        x_t, bx, y_t, by = loaded.pop(ti)
        cond = 1 if ti == 0 else 0
        xo, bxo = xn[i % 2], bxn[i % 2]
        for n in range(8):
            pt, bp = op.next()
            for kc in range(8):
                fw.op("pe", lambda h: h.matmul(pt[:, :W], wo[:, kc, n * 128:(n + 1) * 128], y_t[:, kc, :W],
                                               start=(kc == 0), stop=(kc == 7)),
                      reads=[bwo2[n // 4], by], writes=[bp], inc=(kc == 7))
            fw.op("dve", lambda h: h.scalar_tensor_tensor(out=xo[:, n, :W], in0=pt[:, :W],
                                                          scalar=g.modT[:, l, cond, 16 + n:17 + n], in1=x_t[:, n, :W],
                                                          op0=ALU.mult, op1=ALU.add),
                  reads=[bp, bx, g.bmod], writes=[bxo])
            if side is not None:
                next(side[0], None)
        fw.dma("sp", g.XR[:, t0:t0 + W].rearrange("(kc p) t -> p kc t", p=128), xo[:, :, :W], reads=[bxo], sbuf=bxo)
        if side is not None:
            side[1]()
        side = norm_side(i, ti, t0, W, cond, xo, bxo)
    side[1]()
    ph.close()


def phase_mlp(fw, g, l, last):
    ph = Phase(fw)
    w1 = ph.sb([128, 8, 4096], BF16)
    w2 = ph.sb([128, 32, D], BF16)
    bw1 = [Buf() for _ in range(4)]
    bw2 = [Buf() for _ in range(4)]
    for cb in range(4):
        cast_load3(fw, w1[:, :, cb * 1024:(cb + 1) * 1024], g.w1[l, :, cb * 1024:(cb + 1) * 1024], bw1[cb])
    for fb in range(4):
        cast_load3(fw, w2[:, fb * 8:(fb + 1) * 8, :], g.w2[l, fb * 1024:(fb + 1) * 1024, :], bw2[fb])
    h2 = ph.ring(2, [128, 8, 512], BF16)
    hid = ph.sb([128, 32, 512], BF16)
    bhid = [Buf() for _ in range(32)]
    rr = ph.ring(2, [128, 512], F32)
    xc = ph.ring(4, [128, 512], F32)
    xo = ph.ring(4, [128, 512], F32)
    pp = ph.ring(8, [128, 512], F32, psum=True)
    GI = 4
    tiles = list(enumerate(TILES))
    if last:
        tiles = tiles[1:]
    loaded = {}

    def load(ti):
        t0, W = TILES[ti]
        h_t, bh = h2.next()
        fw.dma("sp", h_t[:, :, :W], g.H2[:, t0:t0 + W].rearrange("(kc p) t -> p kc t", p=128), writes=[bh], sbuf=bh)
        loaded[ti] = (h_t, bh)

    load(tiles[0][0])
    for i, (ti, (t0, W)) in enumerate(tiles):
        if i + 1 < len(tiles):
            load(tiles[i + 1][0])
        h_t, bh = loaded.pop(ti)
        cond = 1 if ti == 0 else 0
        for fg in range(32 // GI):
            pts = [pp.next() for _ in range(GI)]
            for kc in range(8):
                for j in range(GI):
                    f = fg * GI + j
                    pt, bp = pts[j]
                    fw.op("pe", lambda h: h.matmul(pt[:, :W], w1[:, kc, f * 128:(f + 1) * 128], h_t[:, kc, :W],
                                                   start=(kc == 0), stop=(kc == 7)),
                          reads=[bw1[f // 8], bh], writes=[bp], inc=(kc == 7))
            for j in range(GI):
                f = fg * GI + j
                pt, bp = pts[j]
                r_t, br = rr.next()
                fw.op("act", lambda h: h.activation(out=r_t[:, :W], in_=pt[:, :W], func=AF.Relu), reads=[bp], writes=[br])
                fw.op("dve", lambda h: h.tensor_tensor(out=hid[:, f, :W], in0=r_t[:, :W], in1=r_t[:, :W], op=ALU.mult),
                      reads=[br], writes=[bhid[f]])
        for ng in range(8 // GI):
            pts = [pp.next() for _ in range(GI)]
            xcs = []
            for j in range(GI):
                n = ng * GI + j
                x_c, bxc = xc.next()
                fw.dma("sp", x_c[:, :W], g.XR[n * 128:(n + 1) * 128, t0:t0 + W], writes=[bxc], sbuf=bxc)
                xcs.append((x_c, bxc))
            for f in range(32):
                for j in range(GI):
                    n = ng * GI + j
                    pt, bp = pts[j]
                    fw.op("pe", lambda h: h.matmul(pt[:, :W], w2[:, f, n * 128:(n + 1) * 128], hid[:, f, :W],
                                                   start=(f == 0), stop=(f == 31)),
                          reads=[bw2[f // 8], bhid[f]], writes=[bp], inc=(f == 31))
            for j in range(GI):
                n = ng * GI + j
                pt, bp = pts[j]
                x_c, bxc = xcs[j]
                x_o, bxo = xo.next()
                fw.op("dve", lambda h: h.scalar_tensor_tensor(out=x_o[:, :W], in0=pt[:, :W],
                                                              scalar=g.modT[:, l, cond, 40 + n:41 + n], in1=x_c[:, :W],
                                                              op0=ALU.mult, op1=ALU.add),
                      reads=[bp, bxc, g.bmod], writes=[bxo])
                fw.dma("sp", g.XR[n * 128:(n + 1) * 128, t0:t0 + W], x_o[:, :W], reads=[bxo], sbuf=bxo)
    ph.close()


def phase_final(fw, g):
    ph = Phase(fw)
    xt = ph.ring(2, [128, 8, 512], F32)
    nb = norm_bufs(ph)
    sq, bsq, ssp, bssp, sd, bsd, rstd, brstd, tt, btt = nb
    oo = [ph.sb([128, 8, 512], F32) for _ in range(2)]
    boo = [[Buf() for _ in range(8)] for _ in range(2)]
    ftiles = TILES[1:]

    def fload(i):
        t0, W = ftiles[i]
        x_t, bx = xt.next()
        fw.dma("sp", x_t[:, :, :W], g.XR[:, t0:t0 + W].rearrange("(kc p) t -> p kc t", p=128), writes=[bx], sbuf=bx)
        return (x_t, bx)

    nxt = fload(0)
    for i, (t0, W) in enumerate(ftiles):
        x_t, bx = nxt
        if i + 1 < len(ftiles):
            nxt = fload(i + 1)
        fw.op("act", lambda h: h.activation(out=sq[:, :, :W], in_=x_t[:, :, :W], func=AF.Square), reads=[bx], writes=[bsq])
        rstd_from_sq(fw, g, [sq[:, kc, :W] for kc in range(8)], bsq, g.ones_bf[:, :], ssp, bssp, sd, bsd, rstd, brstd, W, D)
        o_t, bo = oo[i % 2], boo[i % 2]
        for kc in range(8):
            fw.op("dve", lambda h: h.scalar_tensor_tensor(out=o_t[:, kc, :W], in0=x_t[:, kc, :W], scalar=g.fing[:, kc:kc + 1],
                                                          in1=rstd[:, :W], op0=ALU.mult, op1=ALU.mult),
                  reads=[bx, brstd, g.bconst], writes=[bo[kc]])
        fw.dma("sp", g.out[:, t0 - CT:t0 - CT + W].rearrange("(kc p) t -> p kc t", p=128), o_t[:, :, :W], reads=bo, sbuf=bo[0])
    ph.close()


R_AV, R_AG, R_CQ, R_CKV, R_QA, R_QB, R_KA, R_KB, R_V = 0, 256, 512, 768, 896, 1152, 1408, 1536, 1664
R_RQ, R_RK, R_RV, R_GF, R_GB, R_KPA, R_KPB = 1792, 1920, 2048, 2304, 2560, 2816, 2848


def gen_conv(fw, g, ph, pp, l, ctx_out):
    dw = ph.sb([128, 2, 31], F32)
    bdw = Buf()
    fw.dma("sp", dw[:], g.conv_dwT[l, :, :, :], writes=[bdw], sbuf=bdw)
    pw = ph.sb([128, 2, 256], BF16)
    bpw = Buf()
    cast_load3(fw, pw[:, :, :], g.conv_pw[l, :, :], bpw)
    dg = ph.sb([128, 2, 31, 128], BF16)
    bdg = Buf()
    for c in range(2):
        for j in range(31):
            fw.op("dve", lambda h: h.tensor_scalar(out=dg[:, c, j, :], in0=g.ident_bf[:, :], scalar1=dw[:, c, j:j + 1],
                                                   scalar2=None, op0=ALU.mult), reads=[bdw, g.bconst], writes=[bdg])
    ypad = ph.sb([128, 2, L + 30], BF16)
    byp = Buf()
    va = ph.ring(2, [128, 2, 512], BF16)
    ga = ph.ring(2, [128, 2, 512], BF16)
    sgm = ph.ring(2, [128, 2, 512], F32)
    cps = pp
    z2 = [ph.sb([128, 2, 512], F32) for _ in range(2)]
    zs2 = [ph.sb([128, 2, 512], F32) for _ in range(2)]
    bz2 = [[Buf(), Buf()] for _ in range(2)]
    bzs2 = [[Buf(), Buf()] for _ in range(2)]
    op = pp
    mean, msq, var, rstd = [ph.sb([128, 512], F32) for _ in range(4)]
    bmean, bmsq, bvar, brstd = [Buf() for _ in range(4)]
    sd, bsd = rstd, brstd
    dd = ph.sb([128, 2, 512], F32)
    d2 = ph.sb([128, 2, 512], F32)
    bdd = [Buf(), Buf()]
    bd2 = [Buf(), Buf()]
    ss = ph.sb([128, 2, 512], BF16)
    bss = [Buf(), Buf()]
    yo = [ph.sb([128, 2, 512], BF16) for _ in range(2)]
    byo = [[Buf(), Buf()] for _ in range(2)]
    segs = ([(0, CT)] if ctx_out else []) + [(CT, L)]
    it = 0
    for (s0, SL) in segs:
        for c in range(2):
            fw.op("dve", lambda h: h.memset(ypad[:, c, 0:15], 0.0), writes=[byp])
            fw.op("dve", lambda h: h.memset(ypad[:, c, 15 + SL:30 + SL], 0.0), writes=[byp])
        for t0 in range(0, SL, 512):
            W = min(512, SL - t0)
            v_t, bv = va.next()
            g_t, bg = ga.next()
            s_t, bs = sgm.next()
            fw.dma("sp", v_t[:, :, :W], g.UT[R_AV:R_AV + 256, s0 + t0:s0 + t0 + W].rearrange("(c p) t -> p c t", p=128),
                   writes=[bv], sbuf=bv)
            fw.dma("sp", g_t[:, :, :W], g.UT[R_AG:R_AG + 256, s0 + t0:s0 + t0 + W].rearrange("(c p) t -> p c t", p=128),
                   writes=[bg], sbuf=bg)
            fw.op("act", lambda h: h.activation(out=s_t[:, :, :W], in_=g_t[:, :, :W], func=AF.Sigmoid), reads=[bg], writes=[bs])
            fw.op("dve", lambda h: h.tensor_tensor(out=ypad[:, :, 15 + t0:15 + t0 + W], in0=v_t[:, :, :W], in1=s_t[:, :, :W],
                                                   op=ALU.mult), reads=[bv, bs], writes=[byp])
            if (t0 // 512) % 4 == 3:
                yield
        for t0 in range(0, SL, 512):
            W = min(512, SL - t0)
            z, zs, bz, bzs = z2[it % 2], zs2[it % 2], bz2[it % 2], bzs2[it % 2]
            for c in range(2):
                cp, bcp = cps.next()
                for j in range(31):
                    fw.op("pe", lambda h: h.matmul(cp[:, :W], dg[:, c, j, :], ypad[:, c, t0 + j:t0 + j + W],
                                                   start=(j == 0), stop=(j == 30)),
                          reads=[bdg, byp], writes=[bcp], inc=(j == 30))
                fw.op("act", lambda h: h.activation(out=z[:, c, :W], in_=cp[:, :W], func=AF.Identity,
                                                    bias=g.vecs[:, l, c:c + 1], scale=1.0),
                      reads=[bcp, g.bconst], writes=[bz[c]])
                fw.op("act", lambda h: h.activation(out=zs[:, c, :W], in_=z[:, c, :W], func=AF.Square),
                      reads=[bz[c]], writes=[bzs[c]])
            mp, bmp = pp.next()
            sp_, bsp = pp.next()
            for c in range(2):
                fw.op("pe", lambda h: h.matmul(mp[:, :W], g.ones_f[:, :], z[:, c, :W], start=(c == 0), stop=(c == 1)),
                      reads=[bz[c], g.bconst], writes=[bmp], inc=(c == 1))
            for c in range(2):
                fw.op("pe", lambda h: h.matmul(sp_[:, :W], g.ones_f[:, :], zs[:, c, :W], start=(c == 0), stop=(c == 1)),
                      reads=[bzs[c], g.bconst], writes=[bsp], inc=(c == 1))
            fw.op("dve", lambda h: h.tensor_scalar(out=mean[:, :W], in0=mp[:, :W], scalar1=1.0 / 256, scalar2=None, op0=ALU.mult),
                  reads=[bmp], writes=[bmean])
            fw.op("dve", lambda h: h.tensor_tensor(out=msq[:, :W], in0=mean[:, :W], in1=mean[:, :W], op=ALU.mult),
                  reads=[bmean], writes=[bmsq])
            fw.op("dve", lambda h: h.scalar_tensor_tensor(out=var[:, :W], in0=sp_[:, :W], scalar=1.0 / 256, in1=msq[:, :W],
                                                          op0=ALU.mult, op1=ALU.subtract), reads=[bsp, bmsq], writes=[bvar])
            fw.op("act", lambda h: h.activation(out=sd[:, :W], in_=var[:, :W], func=AF.Ln, bias=g.epsb[:, :], scale=1.0),
                  reads=[bvar, g.bconst], writes=[bsd])
            fw.op("act", lambda h: h.activation(out=rstd[:, :W], in_=sd[:, :W], func=AF.Exp, scale=-0.5), reads=[bsd], writes=[brstd])
            for c in range(2):
                fw.op("dve", lambda h: h.tensor_tensor(out=dd[:, c, :W], in0=z[:, c, :W], in1=mean[:, :W], op=ALU.subtract),
                      reads=[bz[c], bmean], writes=[bdd[c]])
                fw.op("dve", lambda h: h.tensor_tensor(out=d2[:, c, :W], in0=dd[:, c, :W], in1=rstd[:, :W], op=ALU.mult),
                      reads=[bdd[c], brstd], writes=[bd2[c]])
                fw.op("act", lambda h: h.activation(out=ss[:, c, :W], in_=d2[:, c, :W], func=AF.Silu,
                                                    bias=g.vecs[:, l, 4 + c:5 + c], scale=g.vecs[:, l, 2 + c:3 + c]),
                      reads=[bd2[c], g.bconst], writes=[bss[c]])
            y_t, by = yo[it % 2], byo[it % 2]
            it += 1
            for co in range(2):
                pt, bp = op.next()
                for ci in range(2):
                    fw.op("pe", lambda h: h.matmul(pt[:, :W], pw[:, ci, co * 128:(co + 1) * 128], ss[:, ci, :W],
                                                   start=(ci == 0), stop=(ci == 1)),
                          reads=[bpw, bss[ci]], writes=[bp], inc=(ci == 1))
                fw.op("act", lambda h: h.activation(out=y_t[:, co, :W], in_=pt[:, :W], func=AF.Copy), reads=[bp], writes=[by[co]])
            fw.dma("sp", g.YC[0:256, s0 + t0:s0 + t0 + W].rearrange("(c p) t -> p c t", p=128), y_t[:, :, :W],
                   reads=by, sbuf=by[0])
            yield


def gen_mla_prep(fw, g, ph, pp, l):
    uqA = ph.sb([128, 2, 384], BF16)
    uqB = ph.sb([128, 2, 384], BF16)
    ukv = ph.sb([128, 4, 128], BF16)
    bw = Buf()
    cast_load3(fw, uqA[:, :, :], g.uqA[l, :, :], bw)
    cast_load3(fw, uqB[:, :, :], g.uqB[l, :, :], bw)
    cast_load(fw, ukv[:].rearrange("p a b -> p (a b)"), g.ukv[l, :, :], bw, 512)
    cq = ph.ring(2, [128, 2, 512], BF16)
    ckv = ph.ring(2, [128, 512], BF16)
    kpa = ph.ring(2, [32, 512], BF16)
    kpb = ph.ring(2, [32, 512], BF16)
    c96 = ph.ring(2, [96, 512], F32)
    s96 = ph.ring(2, [96, 512], F32)
    c32 = ph.ring(2, [32, 512], F32)
    s32 = ph.ring(2, [32, 512], F32)
    sq = ph.sb([128, 3, 512], BF16)
    bsq, bsq2 = Buf(), Buf()
    sd, rstd, sd2, rstd2 = [ph.sb([128, 512], F32) for _ in range(4)]
    bsd, brstd, bsd2, brstd2 = [Buf() for _ in range(4)]
    cqn = ph.sb([128, 2, 512], BF16)
    bcqn = [Buf(), Buf()]
    ckn = ph.sb([128, 512], BF16)
    bckn = Buf()
    pa = pp
    pb = pp
    t1 = ph.ring(2, [96, 512], F32)
    t2 = ph.ring(2, [96, 512], F32)
    qh = ph.ring(3, [96, 512], BF16)
    kh = ph.ring(3, [64, 512], BF16)
    kr = ph.ring(2, [32, 512], BF16)
    vat = ph.ring(3, [128, 4, 128], BF16)
    for (v_t, bv) in vat.items:
        fw.op("dve", lambda h: h.memset(v_t[:, :, 64:128], 1.0), writes=[bv])
    def load(ti):
        t0, W = TILES[ti]
        cq_t, bcq = cq.next()
        ck_t, bck = ckv.next()
        ka_t, bka = kpa.next()
        kb_t, bkb = kpb.next()
        c96t, bc96 = c96.next()
        s96t, bs96 = s96.next()
        c32t, bc32 = c32.next()
        s32t, bs32 = s32.next()
        fw.dma("sp", cq_t[:, :, :W], g.UT[R_CQ:R_CQ + 256, t0:t0 + W].rearrange("(c p) t -> p c t", p=128), writes=[bcq], sbuf=bcq)
        fw.dma("sp", ck_t[:, :W], g.UT[R_CKV:R_CKV + 128, t0:t0 + W], writes=[bck], sbuf=bck)
        fw.dma("sp", ka_t[:, :W], g.UT[R_KPA:R_KPA + 32, t0:t0 + W], writes=[bka], sbuf=bka)
        fw.dma("sp", kb_t[:, :W], g.UT[R_KPB:R_KPB + 32, t0:t0 + W], writes=[bkb], sbuf=bkb)
        fw.dma("sp", c96t[64:96, :W], g.cosm[:, t0:t0 + W], writes=[bc96], sbuf=bc96)
        fw.dma("sp", s96t[64:96, :W], g.sinm[:, t0:t0 + W], writes=[bs96], sbuf=bs96)
        fw.dma("sp", c32t[:, :W], g.cosm[:, t0:t0 + W], writes=[bc32], sbuf=bc32)
        fw.dma("sp", s32t[:, :W], g.sinm[:, t0:t0 + W], writes=[bs32], sbuf=bs32)
        return (cq_t, bcq, ck_t, bck, ka_t, bka, kb_t, bkb, c96t, bc96, s96t, bs96, c32t, bc32, s32t, bs32)

    nxt = load(0)
    for ti, (t0, W) in enumerate(TILES):
        (cq_t, bcq, ck_t, bck, ka_t, bka, kb_t, bkb, c96t, bc96, s96t, bs96, c32t, bc32, s32t, bs32) = nxt
        if ti + 1 < len(TILES):
            nxt = load(ti + 1)
        ssp, bssp = pp.next()
        ssp2, bssp2 = pp.next()
        fw.op("act", lambda h: h.activation(out=sq[:, 0:2, :W], in_=cq_t[:, :, :W], func=AF.Square), reads=[bcq], writes=[bsq])
        fw.op("act", lambda h: h.activation(out=sq[:, 2, :W], in_=ck_t[:, :W], func=AF.Square), reads=[bck], writes=[bsq2])
        rstd_from_sq(fw, g, [sq[:, 0, :W], sq[:, 1, :W]], bsq, g.ones_bf[:, :], ssp, bssp, sd, bsd, rstd, brstd, W, 256)
        rstd_from_sq(fw, g, [sq[:, 2, :W]], bsq2, g.ones_bf[:, :], ssp2, bssp2, sd2, bsd2, rstd2, brstd2, W, 128)
        for kc in range(2):
            fw.op("dve", lambda h: h.scalar_tensor_tensor(out=cqn[:, kc, :W], in0=cq_t[:, kc, :W], scalar=g.vecs[:, l, 6 + kc:7 + kc],
                                                          in1=rstd[:, :W], op0=ALU.mult, op1=ALU.mult),
                  reads=[bcq, brstd, g.bconst], writes=[bcqn[kc]])
        fw.op("dve", lambda h: h.scalar_tensor_tensor(out=ckn[:, :W], in0=ck_t[:, :W], scalar=g.vecs[:, l, 8:9],
                                                      in1=rstd2[:, :W], op0=ALU.mult, op1=ALU.mult),
              reads=[bck, brstd2, g.bconst], writes=[bckn])
        t1t, bt1 = t1.next()
        t2t, bt2 = t2.next()
        kr_t, bkr = kr.next()
        fw.op("dve", lambda h: h.tensor_tensor(out=t1t[:32, :W], in0=ka_t[:, :W], in1=c32t[:, :W], op=ALU.mult), reads=[bka, bc32], writes=[bt1])
        fw.op("dve", lambda h: h.tensor_tensor(out=t2t[:32, :W], in0=kb_t[:, :W], in1=s32t[:, :W], op=ALU.mult), reads=[bkb, bs32], writes=[bt2])
        fw.op("dve", lambda h: h.tensor_tensor(out=kr_t[:, :W], in0=t1t[:32, :W], in1=t2t[:32, :W], op=ALU.add), reads=[bt1, bt2], writes=[bkr])
        for hd in range(4):
            fw.dma("sp", g.KTm[hd, 64:96, t0:t0 + W], kr_t[:, :W], reads=[bkr], sbuf=bkr)
        for hd in range(4):
            pat, bpa = pa.next()
            pbt, bpb = pb.next()
            for kc in range(2):
                fw.op("pe", lambda h: h.matmul(pat[:96, :W], uqA[:, kc, hd * 96:(hd + 1) * 96], cqn[:, kc, :W], start=(kc == 0), stop=(kc == 1)),
                      reads=[bw, bcqn[kc]], writes=[bpa], inc=(kc == 1))
            for kc in range(2):
                fw.op("pe", lambda h: h.matmul(pbt[:96, :W], uqB[:, kc, hd * 96:(hd + 1) * 96], cqn[:, kc, :W], start=(kc == 0), stop=(kc == 1)),
                      reads=[bw, bcqn[kc]], writes=[bpb], inc=(kc == 1))
            q_t, bq = qh.next()
            t1t, bt1 = t1.next()
            t2t, bt2 = t2.next()
            fw.op("act", lambda h: h.activation(out=q_t[0:64, :W], in_=pat[0:64, :W], func=AF.Copy), reads=[bpa], writes=[bq])
            fw.op("dve", lambda h: h.tensor_tensor(out=t1t[64:96, :W], in0=pat[64:96, :W], in1=c96t[64:96, :W], op=ALU.mult), reads=[bpa, bc96], writes=[bt1])
            fw.op("dve", lambda h: h.tensor_tensor(out=t2t[64:96, :W], in0=pbt[64:96, :W], in1=s96t[64:96, :W], op=ALU.mult), reads=[bpb, bs96], writes=[bt2])
            fw.op("dve", lambda h: h.tensor_tensor(out=q_t[64:96, :W], in0=t1t[64:96, :W], in1=t2t[64:96, :W], op=ALU.add), reads=[bt1, bt2], writes=[bq])
            fw.dma("sp", g.QTm[hd, :, t0:t0 + W], q_t[:, :W], reads=[bq], sbuf=bq)
            pat, bpa = pa.next()
            fw.op("pe", lambda h: h.matmul(pat[:64, :W], ukv[:, hd, 0:64], ckn[:, :W], start=True, stop=True), reads=[bw, bckn], writes=[bpa])
            k_t, bk = kh.next()
            fw.op("act", lambda h: h.activation(out=k_t[:, :W], in_=pat[:64, :W], func=AF.Copy), reads=[bpa], writes=[bk])
            fw.dma("sp", g.KTm[hd, 0:64, t0:t0 + W], k_t[:, :W], reads=[bk], sbuf=bk)
        for sub in range(W // 128):
            vpt_, bvp = pp.next()
            vpt = vpt_[:, 0:256].rearrange("p (a b) -> p a b", a=4)
            fw.op("pe", lambda h: h.matmul(vpt[:, :, :], ckn[:, sub * 128:(sub + 1) * 128], ukv[:, :, 64:128], start=True, stop=True),
                  reads=[bw, bckn], writes=[bvp])
            v_t, bv = vat.next()
            fw.op("dve", lambda h: h.tensor_copy(out=v_t[:, :, 0:64], in_=vpt[:, :, :]), reads=[bvp], writes=[bv])
            r0 = t0 + sub * 128
            fw.dma("sp", g.VAm[r0:r0 + 128, :].rearrange("p (a b) -> p a b", a=4), v_t[:, :, :], reads=[bv], sbuf=bv)
        yield


def gen_gqa_prep(fw, g, ph, pp, l):
    xa = ph.ring(2, [128, 3, 512], BF16)
    xb = ph.ring(2, [128, 3, 512], BF16)
    vv = ph.ring(2, [128, 512], BF16)
    cs = ph.ring(2, [128, 512], F32)
    sn = ph.ring(2, [128, 512], F32)
    sq = ph.ring(2, [128, 512], BF16)
    ssp = pp
    sd = ph.ring(2, [128, 512], F32)
    rstd = ph.ring(2, [128, 512], F32)
    t1 = ph.ring(2, [128, 512], F32)
    t2 = ph.ring(2, [128, 512], F32)
    t3 = ph.ring(2, [128, 512], F32)
    qr = ph.ring(3, [128, 512], BF16)
    vat = ph.ring(3, [128, 2, 128], BF16)
    for (v_t, bv) in vat.items:
        fw.op("dve", lambda h: h.memset(v_t[:, :, 64:128], 1.0), writes=[bv])
    def load(ti):
        t0, W = TILES[ti]
        xa_t, bxa = xa.next()
        xb_t, bxb = xb.next()
        v_in, bvi = vv.next()
        cs_t, bcs = cs.next()
        sn_t, bsn = sn.next()
        fw.dma("sp", xa_t[:, 0:2, :W], g.UT[R_QA:R_QA + 256, t0:t0 + W].rearrange("(c p) t -> p c t", p=128), writes=[bxa], sbuf=bxa)
        fw.dma("sp", xa_t[:, 2, :W], g.UT[R_KA:R_KA + 128, t0:t0 + W], writes=[bxa], sbuf=bxa, nodeps=True)
        fw.dma("sp", xb_t[:, 0:2, :W], g.UT[R_QB:R_QB + 256, t0:t0 + W].rearrange("(c p) t -> p c t", p=128), writes=[bxb], sbuf=bxb)
        fw.dma("sp", xb_t[:, 2, :W], g.UT[R_KB:R_KB + 128, t0:t0 + W], writes=[bxb], sbuf=bxb, nodeps=True)
        fw.dma("sp", v_in[:, :W], g.UT[R_V:R_V + 128, t0:t0 + W], writes=[bvi], sbuf=bvi)
        fw.dma("sp", cs_t[:, :W], g.cosg[:, t0:t0 + W], writes=[bcs], sbuf=bcs)
        fw.dma("sp", sn_t[:, :W], g.sing[:, t0:t0 + W], writes=[bsn], sbuf=bsn)
        return (xa_t, bxa, xb_t, bxb, v_in, bvi, cs_t, bcs, sn_t, bsn)

    nxt = load(0)
    for ti, (t0, W) in enumerate(TILES):
        (xa_t, bxa, xb_t, bxb, v_in, bvi, cs_t, bcs, sn_t, bsn) = nxt
        if ti + 1 < len(TILES):
            nxt = load(ti + 1)
        for c in range(3):
            ga, gb = (9, 10) if c < 2 else (11, 12)
            sq_t, bsq = sq.next()
            ss_t, bss = ssp.next()
            sd_t, bsd = sd.next()
            rs_t, brs = rstd.next()
            fw.op("act", lambda h: h.activation(out=sq_t[:, :W], in_=xa_t[:, c, :W], func=AF.Square), reads=[bxa], writes=[bsq])
            rstd_from_sq(fw, g, [sq_t[:, :W]], bsq, g.blk_bf[:, :], ss_t, bss, sd_t, bsd, rs_t, brs, W, 64)
            a_t, ba = t1.next()
            b_t, bb = t2.next()
            c_t, bc = t3.next()
            q_t, bq = qr.next()
            fw.op("dve", lambda h: h.scalar_tensor_tensor(out=a_t[:, :W], in0=xa_t[:, c, :W], scalar=g.vecs[:, l, ga:ga + 1], in1=cs_t[:, :W],
                                                          op0=ALU.mult, op1=ALU.mult), reads=[bxa, bcs, g.bconst], writes=[ba])
            fw.op("dve", lambda h: h.scalar_tensor_tensor(out=b_t[:, :W], in0=xb_t[:, c, :W], scalar=g.vecs[:, l, gb:gb + 1], in1=sn_t[:, :W],
                                                          op0=ALU.mult, op1=ALU.mult), reads=[bxb, bsn, g.bconst], writes=[bb])
            fw.op("pool", lambda h: h.tensor_tensor(out=c_t[:, :W], in0=a_t[:, :W], in1=b_t[:, :W], op=ALU.add), reads=[ba, bb], writes=[bc])
            fw.op("pool", lambda h: h.tensor_tensor(out=q_t[:, :W], in0=c_t[:, :W], in1=rs_t[:, :W], op=ALU.mult), reads=[bc, brs], writes=[bq])
            if c < 2:
                fw.dma("sp", g.QTg[c, :, t0:t0 + W], q_t[0:64, :W], reads=[bq], sbuf=bq)
                fw.dma("sp", g.QTg[c + 2, :, t0:t0 + W], q_t[64:128, :W], reads=[bq], sbuf=bq)
            else:
                fw.dma("sp", g.KTg[0, :, t0:t0 + W], q_t[0:64, :W], reads=[bq], sbuf=bq)
                fw.dma("sp", g.KTg[1, :, t0:t0 + W], q_t[64:128, :W], reads=[bq], sbuf=bq)
        for sub in range(W // 128):
            vpt_, bvp = pp.next()
            vpt = vpt_[:, 0:128].rearrange("p (a b) -> p a b", a=2)
            fw.op("pe", lambda h: h.matmul(vpt[:, :, :], v_in[:, sub * 128:(sub + 1) * 128], g.ident_bf[:, :], start=True, stop=True),
                  reads=[bvi, g.bconst], writes=[bvp])
            v_t, bv = vat.next()
            fw.op("dve", lambda h: h.tensor_copy(out=v_t[:, :, 0:64], in_=vpt[:, :, :]), reads=[bvp], writes=[bv])
            r0 = t0 + sub * 128
            fw.dma("sp", g.VAg[r0:r0 + 128, :].rearrange("p (a b) -> p a b", a=2), v_t[:, :, :], reads=[bv], sbuf=bv)
        yield


def phase_prep(fw, g, l, ctx_out):
    ph = Phase(fw)
    pp = ph.ring(8, [128, 512], F32, psum=True)
    gens = [gen_conv(fw, g, ph, pp, l, ctx_out), gen_mla_prep(fw, g, ph, pp, l), gen_gqa_prep(fw, g, ph, pp, l)]
    while gens:
        for gn in list(gens):
            try:
                next(gn)
            except StopIteration:
                gens.remove(gn)
    ph.close()


def phase_attn(fw, g, QT, KT, VA, nheads, nkv, dk, scale, yrow0, ctx_out):
    ph = Phase(fw)
    pad = (dk == 64)
    if pad:
        qT = [ph.sb([128, T], BF16) for _ in range(2)]
        kT = [ph.sb([128, T], BF16) for _ in range(nkv)]
        bq = [Buf() for _ in range(2)]
        bk = [Buf() for _ in range(nkv)]
        for kv in range(nkv):
            fw.op("pool", lambda h: h.memset(kT[kv][(1 - kv) * 64:(2 - kv) * 64, :], 0.0), writes=[bk[kv]])
            fw.dma("sp", kT[kv][kv * 64:(kv + 1) * 64, :], KT[kv, :, :], writes=[bk[kv]], sbuf=bk[kv], nodeps=True)
        for c in range(2):
            fw.dma("sp", qT[c][0:64, :], QT[c, :, :], writes=[bq[c]], sbuf=bq[c])
            fw.dma("sp", qT[c][64:128, :], QT[c + 2, :, :], writes=[bq[c]], sbuf=bq[c], nodeps=True)
        KD = 128
    else:
        qT = [ph.sb([dk, T], BF16) for _ in range(nheads)]
        kT = [ph.sb([dk, T], BF16) for _ in range(nkv)]
        bq = [Buf() for _ in range(nheads)]
        bk = [Buf() for _ in range(nkv)]
        fw.dma("sp", kT[0][:, :], KT[0, :, :], writes=[bk[0]], sbuf=bk[0])
        fw.dma("sp", qT[0][:, :], QT[0, :, :], writes=[bq[0]], sbuf=bq[0])
        KD = dk
    va = ph.sb([128, NKT, nkv * 128], BF16)
    bva = Buf()
    fw.dma("sp", va[:, :, :], VA[:, :].rearrange("(k p) c -> p k c", p=128), writes=[bva], sbuf=bva)
    if not pad:
        for kv in range(1, nkv):
            fw.dma("sp", kT[kv][:, :], KT[kv, :, :], writes=[bk[kv]], sbuf=bk[kv])
        for hd in range(1, nheads):
            fw.dma("sp", qT[hd][:, :], QT[hd, :, :], writes=[bq[hd]], sbuf=bq[hd])
    s_ps = ph.ring(3, [128, 2, 512], F32, psum=True)
    p_sb = ph.ring(4, [128, 2, 512], BF16)
    o_ps = ph.ring(1, [128, 512], F32, psum=True)
    bc_ps = ph.ps()
    bbc = Buf()
    o_sb = ph.ring(2, [128, 512], F32)
    rbc = ph.sb([64, 512], F32)
    brbc = Buf()
    yh = ph.ring(2, [64, 512], BF16)
    steps = []
    for hd in range(nheads):
        kv = hd * nkv // nheads
        for ti, (t0, W) in enumerate(TILES):
            if ti == 0 and not ctx_out:
                continue
            nk = 2 if ti == 0 else NKT
            for kp in range(nk // 2):
                steps.append((hd, kv, t0, W, kp, nk // 2))
    LA = 2
    deferred = []
    pend = {}
    cur_o = [None]

    def emit_s(i):
        hd, kv, t0, W, kp, npair = steps[i]
        qi = (hd % 2) if pad else hd
        st, bs = s_ps.next()
        for j in range(2):
            kt = 2 * kp + j
            fw.op("pe", lambda h: h.matmul(st[:, j, :W], kT[kv][:KD, kt * 128:(kt + 1) * 128], qT[qi][:KD, t0:t0 + W], start=True, stop=True),
                  reads=[bk[kv], bq[qi]], writes=[bs], inc=(j == 1))
        pt, bp = p_sb.next()
        fw.op("act", lambda h: h.activation(out=pt[:, :, :W], in_=st[:, :, :W], func=AF.Exp, scale=scale), reads=[bs], writes=[bp])
        pend[i] = (pt, bp)

    def emit_pv(i):
        hd, kv, t0, W, kp, npair = steps[i]
        pt, bp = pend.pop(i)
        if kp == 0:
            cur_o[0] = o_ps.next()
        ot, bo = cur_o[0]
        for j in range(2):
            kt = 2 * kp + j
            first = (kp == 0 and j == 0)
            lastm = (kp == npair - 1 and j == 1)
            fw.op("pe", lambda h: h.matmul(ot[:, :W], va[:, kt, kv * 128:(kv + 1) * 128], pt[:, j, :W], start=first, stop=lastm),
                  reads=[bva, bp], writes=[bo], inc=lastm)
        if kp == npair - 1:
            os_t, bos = o_sb.next()
            fw.op("dve", lambda h: h.tensor_copy(out=os_t[:65, :W], in_=ot[:65, :W]), reads=[bo], writes=[bos])

            def fin(os_t=os_t, bos=bos, hd=hd, t0=t0, W=W):
                fw.op("pe", lambda h: h.matmul(bc_ps[:64, :W], g.ones_f[64:65, 0:64], os_t[64:65, :W], start=True, stop=True),
                      reads=[bos, g.bconst], writes=[bbc])
                fw.op("dve", lambda h: h.reciprocal(out=rbc[:, :W], in_=bc_ps[:64, :W]), reads=[bbc], writes=[brbc])
                y_t, by = yh.next()
                fw.op("dve", lambda h: h.tensor_tensor(out=y_t[:, :W], in0=os_t[:64, :W], in1=rbc[:, :W], op=ALU.mult), reads=[bos, brbc], writes=[by])
                fw.dma("sp", g.YC[yrow0 + hd * 64:yrow0 + (hd + 1) * 64, t0:t0 + W], y_t[:, :W], reads=[by], sbuf=by)
            deferred.append(fin)

    n = len(steps)
    for i in range(n + LA):
        todo = list(deferred)
        del deferred[:]
        if i < n:
            emit_s(i)
        if i >= LA:
            emit_pv(i - LA)
        for f in todo:
            f()
    for f in deferred:
        f()
    ph.close()


def phase_ret_prep(fw, g, l, mod_next=None):
    ph = Phase(fw)
    e1 = ph.sb([128, 8], F32)
    lg = ph.sb([128, 8], F32)
    be1, blg = Buf(), Buf()
    fw.op("act", lambda h: h.activation(out=e1[:, :], in_=g.vecs[:, l, 21:29], func=AF.Exp, scale=-1.0), reads=[g.bconst], writes=[be1])
    fw.op("act", lambda h: h.activation(out=lg[:, :], in_=e1[:, :], func=AF.Ln, bias=g.oneb[:, :], scale=1.0), reads=[be1, g.bconst], writes=[blg])
    fw.op("dve", lambda h: h.tensor_scalar(out=g.lg[:, l, :], in0=lg[:, :], scalar1=-1.0, scalar2=None, op0=ALU.mult), reads=[blg], writes=[g.blg])
    fk = ph.sb([128, 2, 4, 32], F32)
    bfk = Buf()
    tmp = ph.sb([128, 8], F32)
    btmp = Buf()
    for d in range(2):
        src = g.c_rev if d == 0 else g.c_pidx
        for hd in range(4):
            j = d * 4 + hd
            fw.op("act", lambda h: h.activation(out=tmp[:, j:j + 1], in_=src, func=AF.Exp, scale=g.lg[:, l, j:j + 1]),
                  reads=[g.blg, g.bconst], writes=[btmp])
            fw.op("dve", lambda h: h.tensor_scalar(out=fk[:, d, hd, :], in0=g.ones_f[:, 0:32], scalar1=tmp[:, j:j + 1], scalar2=32.0 ** -0.5,
                                                   op0=ALU.mult, op1=ALU.mult), reads=[btmp, g.bconst], writes=[bfk])
    rk = ph.sb([128, T], BF16)
    rv = ph.sb([128, 2, T], BF16)
    brk, brv = Buf(), Buf()
    fw.dma("sp", rk[:, :], g.UT[R_RK:R_RK + 128, :], writes=[brk], sbuf=brk)
    fw.dma("sp", rv[:, :, :], g.UT[R_RV:R_RV + 256, :].rearrange("(c p) t -> p c t", p=128), writes=[brv], sbuf=brv)
    kp = ph.ring(2, [128, 128], F32, psum=True)
    vp = ph.ring(2, [128, 256], F32, psum=True)
    kd = ph.ring(3, [128, 2, 128], BF16)
    vt = ph.ring(3, [128, 256], BF16)
    def chunk(c):
        kpt, bkp = kp.next()
        fw.op("pe", lambda h: h.matmul(kpt[:, :], rk[:, c * 128:(c + 1) * 128], g.ident_bf[:, :], start=True, stop=True),
              reads=[brk, g.bconst], writes=[bkp])
        kd_t, bkd = kd.next()
        for d in range(2):
            fw.op("dve", lambda h: h.tensor_tensor(out=kd_t[:, d, :], in0=kpt[:, :], in1=fk[:, d, :, :].rearrange("p a b -> p (a b)"), op=ALU.mult),
                  reads=[bkp, bfk], writes=[bkd])
        fw.dma("sp", g.KDEC[:, c * 128:(c + 1) * 128, :].rearrange("d p f -> p d f"), kd_t[:, :, :], reads=[bkd], sbuf=bkd)
        vpt, bvp = vp.next()
        for cc in range(2):
            fw.op("pe", lambda h: h.matmul(vpt[:, cc * 128:(cc + 1) * 128], rv[:, cc, c * 128:(c + 1) * 128], g.ident_bf[:, :], start=True, stop=True),
                  reads=[brv, g.bconst], writes=[bvp], inc=(cc == 1))
        v_t, bv = vt.next()
        fw.op("act", lambda h: h.activation(out=v_t[:, :], in_=vpt[:, :], func=AF.Copy), reads=[bvp], writes=[bv])
        fw.dma("sp", g.VTOK[c * 128:(c + 1) * 128, :], v_t[:, :], reads=[bv], sbuf=bv)
    gt = ph.ring(2, [64, T], BF16)
    st = ph.ring(2, [64, T], F32)
    so = ph.ring(2, [64, T], BF16)
    gates = [(d, hd) for d in range(2) for hd in range(4)]
    gl = {}

    def gload(i):
        d, hd = gates[i]
        r0 = R_GF if d == 0 else R_GB
        g_t, bg = gt.next()
        fw.dma("sp", g_t[:, :], g.UT[r0 + hd * 64:r0 + (hd + 1) * 64, :], writes=[bg], sbuf=bg)
        gl[i] = (g_t, bg)

    def gproc(i):
        d, hd = gates[i]
        g_t, bg = gl.pop(i)
        s_t, bs = st.next()
        o_t, bo = so.next()
        fw.op("act", lambda h: h.activation(out=s_t[:, :], in_=g_t[:, :], func=AF.Silu), reads=[bg], writes=[bs])
        fw.op("dve", lambda h: h.tensor_scalar(out=o_t[:, :], in0=s_t[:, :], scalar1=g.vecs[0:64, l, 13 + d * 4 + hd:14 + d * 4 + hd], scalar2=None,
                                               op0=ALU.mult), reads=[bs, g.bconst], writes=[bo])
        fw.dma("sp", g.SG[d, hd, :, :], o_t[:, :], reads=[bo], sbuf=bo)

    gload(0)
    ci = 0
    for i in range(8):
        if i + 1 < 8:
            gload(i + 1)
        gproc(i)
        for _ in range(4):
            if ci < NKT:
                chunk(ci)
                ci += 1
    while ci < NKT:
        chunk(ci)
        ci += 1
    if mod_next is not None:
        emit_mod(fw, g, ph, mod_next)
    ph.close()


def phase_ret_scan(fw, g, l, ctx_out):
    ph = Phase(fw)
    qT = [ph.sb([32, T], BF16) for _ in range(4)]
    kT = [ph.sb([32, T], BF16) for _ in range(4)]
    bq = [Buf() for _ in range(4)]
    bk = [Buf() for _ in range(4)]
    for hd in range(4):
        fw.dma("sp", qT[hd][:, :], g.UT[R_RQ + hd * 32:R_RQ + (hd + 1) * 32, :], writes=[bq[hd]], sbuf=bq[hd])
        fw.dma("sp", kT[hd][:, :], g.UT[R_RK + hd * 32:R_RK + (hd + 1) * 32, :], writes=[bk[hd]], sbuf=bk[hd])
    vtok = ph.sb([128, NKT, 256], BF16)
    kdec = ph.sb([128, 2, NKT, 128], BF16)
    bvt, bkdec = Buf(), Buf()
    fw.dma("sp", vtok[:, :, :], g.VTOK[:, :].rearrange("(c p) f -> p c f", p=128), writes=[bvt], sbuf=bvt)
    for d in range(2):
        fw.dma("sp", kdec[:, d, :, :], g.KDEC[d, :, :].rearrange("(c p) f -> p c f", p=128), writes=[bkdec], sbuf=bkdec, nodeps=True)
    maskT = ph.sb([128, 2, 4, 128], F32)
    fq = ph.sb([32, 2, 4, 128], F32)
    g128 = ph.sb([32, 8], F32)
    bmask, bfq, bg128 = Buf(), Buf(), Buf()
    s = 32.0 ** -0.5
    for d in range(2):
        dm = g.c_dmat if d == 0 else g.c_dmatn
        mk = g.c_mge if d == 0 else g.c_mle
        qi = g.c_ip1 if d == 0 else g.c_128mi
        for hd in range(4):
            j = d * 4 + hd
            fw.op("act", lambda h: h.activation(out=maskT[:, d, hd, :], in_=dm, func=AF.Exp, scale=g.lg[:, l, j:j + 1]),
                  reads=[g.blg, g.bconst], writes=[bmask])
            fw.op("dve", lambda h: h.scalar_tensor_tensor(out=maskT[:, d, hd, :], in0=maskT[:, d, hd, :], scalar=s, in1=mk,
                                                          op0=ALU.mult, op1=ALU.mult), reads=[bmask, g.bconst], writes=[bmask])
            fw.op("act", lambda h: h.activation(out=fq[:, d, hd, :], in_=qi[0:32, :], func=AF.Exp, scale=g.lg[0:32, l, j:j + 1]),
                  reads=[g.blg, g.bconst], writes=[bfq])
    fw.op("act", lambda h: h.activation(out=g128[:, :], in_=g.lg[0:32, l, :], func=AF.Exp, scale=128.0), reads=[g.blg], writes=[bg128])
    S32 = ph.sb([32, 2, 4, 64], F32)
    Sbf = ph.sb([32, 2, 2, 4, 64], BF16)
    bS = [Buf(), Buf()]
    bSbf = [[Buf(), Buf()], [Buf(), Buf()]]
    for d in range(2):
        fw.op("dve", lambda h: h.memset(S32[:, d, :, :], 0.0), writes=[bS[d]])
        fw.op("dve", lambda h: h.memset(Sbf[:, d, 0, :, :], 0.0), writes=[bSbf[d][0]])
    yf = ph.sb([64, 4, T], BF16)
    byf = [Buf() for _ in range(NKT)]
    sc_ps = ph.ring(2, [128, 4, 128], F32, psum=True)
    o_ps = ph.ring(2, [64, 4, 128], F32, psum=True)
    su_ps = ph.ring(2, [32, 4, 64], F32, psum=True)
    ss_ps = ph.ring(2, [64, 512], F32, psum=True)
    sc = ph.ring(2, [128, 4, 128], BF16)
    qd = ph.ring(2, [32, 4, 128], BF16)
    osb = ph.ring(6, [64, 4, 128], F32)
    sqo = ph.ring(2, [64, 4, 128], BF16)
    lnv = ph.ring(2, [64, 512], F32)
    rst = ph.ring(4, [64, 512], F32)
    on = ph.ring(2, [64, 4, 128], F32)
    sg = ph.ring(6, [64, 4, 128], BF16)
    yb = ph.ring(2, [64, 4, 128], F32)
    yo = ph.ring(2, [64, 4, 128], BF16)
    order_f = list(range(NKT))
    order_b = [1, 0] + list(range(NKT - 1, 1, -1))
    visited = set()

    def stage_a(k, d, c):
        c0 = c * 128
        need_out = ctx_out or c >= 2
        st = dict(d=d, c=c, need_out=need_out)
        if need_out:
            sg_t, bsg = sg.next()
            fw.dma("sp", sg_t[:, :, :], g.SG[d, :, :, c0:c0 + 128].rearrange("h p t -> p h t"), writes=[bsg], sbuf=bsg)
            st["sg"] = (sg_t, bsg)
            sct, bsc = sc_ps.next()
            for hd in range(4):
                fw.op("pe", lambda h: h.matmul(sct[:, hd, :], kT[hd][:, c0:c0 + 128], qT[hd][:, c0:c0 + 128], start=True, stop=True),
                      reads=[bk[hd], bq[hd]], writes=[bsc], inc=(hd == 3))
            sc_t, bscs = sc.next()
            fw.op("dve", lambda h: h.tensor_tensor(out=sc_t[:, :, :], in0=sct[:, :, :], in1=maskT[:, d, :, :], op=ALU.mult),
                  reads=[bsc, bmask], writes=[bscs])
            qd_t, bqd = qd.next()
            for hd in range(4):
                fw.op("pool", lambda h: h.tensor_tensor(out=qd_t[:, hd, :], in0=qT[hd][:, c0:c0 + 128], in1=fq[:, d, hd, :], op=ALU.mult),
                      reads=[bq[hd], bfq], writes=[bqd])
        sut, bsu = su_ps.next()
        for hd in range(4):
            fw.op("pe", lambda h: h.matmul(sut[:, hd, :], kdec[:, d, c, hd * 32:(hd + 1) * 32], vtok[:, c, hd * 64:(hd + 1) * 64], start=True, stop=True),
                  reads=[bkdec, bvt], writes=[bsu], inc=(hd == 3))
        if need_out:
            ot, bo = o_ps.next()
            for hd in range(4):
                fw.op("pe", lambda h: h.matmul(ot[:, hd, :], vtok[:, c, hd * 64:(hd + 1) * 64], sc_t[:, hd, :], start=True, stop=False),
                      reads=[bvt, bscs], writes=[bo], inc=False)
                fw.op("pe", lambda h: h.matmul(ot[:, hd, :], Sbf[:, d, k % 2, hd, :], qd_t[:, hd, :], start=False, stop=True),
                      reads=[bSbf[d][k % 2], bqd], writes=[bo], inc=(hd == 3))
            os_t, bos = osb.next()
            fw.op("act", lambda h: h.activation(out=os_t[:, :, :], in_=ot[:, :, :], func=AF.Copy), reads=[bo], writes=[bos])
            st["o"] = (os_t, bos)
        for hd in range(4):
            fw.op("dve", lambda h: h.scalar_tensor_tensor(out=S32[:, d, hd, :], in0=S32[:, d, hd, :], scalar=g128[:, d * 4 + hd:d * 4 + hd + 1],
                                                          in1=sut[:, hd, :], op0=ALU.mult, op1=ALU.add),
                  reads=[bS[d], bg128, bsu], writes=[bS[d]])
        fw.op("dve", lambda h: h.tensor_copy(out=Sbf[:, d, (k + 1) % 2, :, :], in_=S32[:, d, :, :]), reads=[bS[d]], writes=[bSbf[d][(k + 1) % 2]])
        return st

    def stage_b(st):
        if not st["need_out"]:
            return
        os_t, bos = st["o"]
        sq_t, bsq = sqo.next()
        fw.op("act", lambda h: h.activation(out=sq_t[:, :, :], in_=os_t[:, :, :], func=AF.Square), reads=[bos], writes=[bsq])
        sst, bss = ss_ps.next()
        fw.op("pe", lambda h: h.matmul(sst[:, :], g.ones_bf[0:64, 0:64], sq_t[:, :, :].rearrange("p a b -> p (a b)"), start=True, stop=True),
              reads=[bsq, g.bconst], writes=[bss])
        ln_t, bln = lnv.next()
        rs_t, brs = rst.next()
        fw.op("act", lambda h: h.activation(out=ln_t[:, :], in_=sst[:, :], func=AF.Ln, bias=g.epsb[0:64, :], scale=1.0 / 64), reads=[bss, g.bconst], writes=[bln])
        fw.op("act", lambda h: h.activation(out=rs_t[:, :], in_=ln_t[:, :], func=AF.Exp, scale=-0.5), reads=[bln], writes=[brs])
        st["rs"] = (rs_t, brs)

    def stage_c(st):
        if not st["need_out"]:
            return
        d, c = st["d"], st["c"]
        c0 = c * 128
        os_t, bos = st["o"]
        rs_t, brs = st["rs"]
        sg_t, bsg = st["sg"]
        on_t, bon = on.next()
        fw.op("dve", lambda h: h.tensor_tensor(out=on_t[:, :, :], in0=os_t[:, :, :], in1=rs_t[:, :].rearrange("p (a b) -> p a b", a=4), op=ALU.mult),
              reads=[bos, brs], writes=[bon])
        if c not in visited:
            visited.add(c)
            fw.op("pool", lambda h: h.tensor_tensor(out=yf[:, :, c0:c0 + 128], in0=on_t[:, :, :], in1=sg_t[:, :, :], op=ALU.mult),
                  reads=[bon, bsg], writes=[byf[c]])
        else:
            yb_t, byb = yb.next()
            yo_t, byo = yo.next()
            fw.op("pool", lambda h: h.tensor_tensor(out=yb_t[:, :, :], in0=on_t[:, :, :], in1=sg_t[:, :, :], op=ALU.mult),
                  reads=[bon, bsg], writes=[byb])
            fw.op("pool", lambda h: h.tensor_tensor(out=yo_t[:, :, :], in0=yb_t[:, :, :], in1=yf[:, :, c0:c0 + 128], op=ALU.add),
                  reads=[byb, byf[c]], writes=[byo])
            fw.dma("sp", g.YC[768:1024, c0:c0 + 128].rearrange("(h p) t -> p h t", p=64), yo_t[:, :, :], reads=[byo], sbuf=byo)

    hist = {}
    for k in range(NKT + 2):
        if k < NKT:
            hist[k] = [stage_a(k, 0, order_f[k]), stage_a(k, 1, order_b[k])]
        if 0 <= k - 1 < NKT:
            for st in hist[k - 1]:
                stage_b(st)
        if 0 <= k - 2 < NKT:
            for st in hist.pop(k - 2):
                stage_c(st)
    ph.close()


NCONST = 6 * 128 + 2


def build(n_layers=DEPTH, taps=(), stop_after=None, layer_list=None):
    nc = bass.Bass("TRN2", target_bir_lowering=False)
    g = G()

    def din(name, shape, dt=F32):
        return nc.dram_tensor(name, list(shape), dt, kind="ExternalInput").ap()

    def dscr(name, shape, dt):
        kind = "ExternalOutput" if name in taps else "Internal"
        return nc.dram_tensor(name, list(shape), dt, kind=kind).ap()

    g.xin = din("xin", [D, T])
    g.cc = din("cc", [128, 8, 2])
    g.w_mod = din("w_mod", [DEPTH, D, 6 * D])
    g.b_modT = din("b_modT", [128, DEPTH, 48])
    g.w_in = din("w_in", [DEPTH, D, NU])
    g.w_out = din("w_out", [DEPTH, D, D])
    g.conv_dwT = din("conv_dwT", [DEPTH, 128, 2, 31])
    g.conv_pw = din("conv_pw", [DEPTH, 256, 256])
    g.uqA = din("uqA", [DEPTH, 256, 384])
    g.uqB = din("uqB", [DEPTH, 256, 384])
    g.ukv = din("ukv", [DEPTH, 128, 512])
    g.w1 = din("w1", [DEPTH, D, 4 * D])
    g.w2 = din("w2", [DEPTH, 4 * D, D])
    g.vecs_d = din("vecs", [128, DEPTH, NV])
    g.fing_d = din("fing", [128, 8])
    g.cosg = din("cosg", [128, T])
    g.sing = din("sing", [128, T])
    g.cosm = din("cosm", [32, T])
    g.sinm = din("sinm", [32, T])
    g.ident_d = din("ident", [128, 128])
    g.blk_d = din("blk", [128, 128])
    g.consts_d = din("consts", [128, NCONST])
    g.out = nc.dram_tensor("out", [D, L], F32, kind="ExternalOutput").ap()
    g.XR = dscr("XR", [D, T], F32)
    g.UT = dscr("UT", [NU, T], BF16)
    g.YC = dscr("YC", [D, T], BF16)
    g.H2 = dscr("H2", [D, T], BF16)
    g.QTm = dscr("QTm", [4, 96, T], BF16)
    g.KTm = dscr("KTm", [4, 96, T], BF16)
    g.VAm = dscr("VAm", [T, 4 * 128], BF16)
    g.QTg = dscr("QTg", [4, 64, T], BF16)
    g.KTg = dscr("KTg", [2, 64, T], BF16)
    g.VAg = dscr("VAg", [T, 2 * 128], BF16)
    g.KDEC = dscr("KDEC", [2, T, 128], BF16)
    g.VTOK = dscr("VTOK", [T, 256], BF16)
    g.SG = dscr("SG", [2, 4, 64, T], BF16)
    g.MODT = dscr("MODT", [128, DEPTH * 2 * 48], F32)

    fw = FW(nc)
    res = ExitStack()

    def rsb(name, shape, dt):
        return res.enter_context(nc.sbuf_tensor(name, list(shape), dt))

    g.modT = rsb("modT", [128, DEPTH, 2, 48], F32)
    g.vecs = rsb("vecs_sb", [128, DEPTH, NV], F32)
    g.fing = rsb("fing_sb", [128, 8], F32)
    g.consts = rsb("consts_sb", [128, NCONST], F32)
    g.ident_bf = rsb("ident_bf", [128, 128], BF16)
    g.blk_bf = rsb("blk_bf", [128, 128], BF16)
    g.ones_bf = rsb("ones_bf", [128, 128], BF16)
    g.ones_f = rsb("ones_f", [128, 128], F32)
    g.epsb = rsb("epsb", [128, 1], F32)
    g.oneb = rsb("oneb", [128, 1], F32)
    g.lg = rsb("lg_sb", [128, DEPTH, 8], F32)
    g.bm = rsb("bm_sb", [128, DEPTH, 48], F32)
    g.sl_bf = rsb("sl_bf", [128, 8, 2], BF16)
    g.ident_f = rsb("ident_f", [128, 128], F32)
    g.bmod, g.bconst, g.blg = Buf(), Buf(), Buf()
    fw.persistent = [g.bmod, g.bconst, g.blg]
    cs = g.consts
    g.c_dmat = cs[:, 0:128]
    g.c_dmatn = cs[:, 128:256]
    g.c_mge = cs[:, 256:384]
    g.c_mle = cs[:, 384:512]
    g.c_ip1 = cs[:, 512:640]
    g.c_128mi = cs[:, 640:768]
    g.c_rev = cs[:, 768:769]
    g.c_pidx = cs[:, 769:770]

    ph = Phase(fw)
    b1, b2, b3, b4, b5 = Buf(), Buf(), Buf(), Buf(), Buf()
    fw.dma("sp", g.vecs[:, :, :], g.vecs_d[:, :, :], writes=[b1], sbuf=b1)
    fw.dma("sp", g.fing[:, :], g.fing_d[:, :], writes=[b2], sbuf=b2)
    fw.dma("sp", g.consts[:, :], g.consts_d[:, :], writes=[b3], sbuf=b3)
    fw.dma("pool", g.ident_bf[:, :], g.ident_d[:, :], writes=[b4], sbuf=b4)
    fw.dma("pool", g.blk_bf[:, :], g.blk_d[:, :], writes=[b5], sbuf=b5)
    fw.op("dve", lambda h: h.memset(g.ones_bf[:, :], 1.0))
    fw.op("dve", lambda h: h.memset(g.ones_f[:, :], 1.0))
    fw.op("dve", lambda h: h.memset(g.epsb[:, :], EPS))
    fw.op("dve", lambda h: h.memset(g.oneb[:, :], 1.0))
    b6, b7, b8 = Buf(), Buf(), Buf()
    cc_sb = ph.sb([128, 8, 2], F32)
    fw.dma("sp", cc_sb[:], g.cc[:, :, :], writes=[b6], sbuf=b6)
    fw.dma("sp", g.bm[:, :, :], g.b_modT[:, :, :], writes=[b7], sbuf=b7)
    fw.dma("sp", g.ident_f[:, :], g.ident_d[:, :], writes=[b8], sbuf=b8)
    fw.op("act", lambda h: h.activation(out=g.sl_bf[:], in_=cc_sb[:], func=AF.Silu), reads=[b6])
    ph.close()

    def done():
        res.close()
        fw.es.close()
        return nc

    ph = Phase(fw)
    first_l = layer_list[0] if layer_list is not None else 0
    emit_mod(fw, g, ph, first_l)
    ph.close()
    if "MODT" in taps:
        ph = Phase(fw)
        bt = Buf()
        fw.dma("sp", g.MODT[:, :], g.modT[:, :, :, :].rearrange("p a b c -> p (a b c)"), sbuf=bt)
        ph.close()
    if stop_after == "mod":
        return done()
    for l in (layer_list if layer_list is not None else range(n_layers)):
        last = (l == DEPTH - 1)
        ctx_out = not last
        xsrc = g.xin if l == 0 else g.XR
        phase_inproj(fw, g, l, xsrc)
        if stop_after == "inproj":
            return done()
        phase_prep(fw, g, l, ctx_out)
        if stop_after == "conv":
            return done()
        phase_attn(fw, g, g.QTm, g.KTm, g.VAm, 4, 4, 96, 96.0 ** -0.5, 256, ctx_out)
        if stop_after == "mla":
            return done()
        phase_attn(fw, g, g.QTg, g.KTg, g.VAg, 4, 2, 64, 64.0 ** -0.5, 512, ctx_out)
        if stop_after == "gqa":
            return done()
        phase_ret_prep(fw, g, l, mod_next=(l + 1 if (layer_list is None and l + 1 < n_layers) else None))
        phase_ret_scan(fw, g, l, ctx_out)
        if stop_after == "ret":
            return done()
        phase_outproj(fw, g, l, xsrc, last)
        if stop_after == "outproj":
            return done()
        phase_mlp(fw, g, l, last)
    phase_final(fw, g)
    return done()


def _rope_tables(ndim):
    half2 = ndim // 2
    hh = half2 // 2
    t = np.arange(L)
    row = (t // 64).astype(np.float32)
    col = (t % 64).astype(np.float32)
    freqs = (np.float32(10000.0) ** (-np.arange(hh, dtype=np.float32) / np.float32(hh))).astype(np.float32)
    cos = np.ones((ndim, T), np.float32)
    sin = np.zeros((ndim, T), np.float32)
    partner = np.zeros(ndim, np.int64)
    for d in range(ndim):
        blk = d // half2
        r = d % half2
        first = r < hh
        i = r % hh
        pos = row if blk == 0 else col
        ang = (pos * freqs[i]).astype(np.float32)
        cos[d, CT:] = np.cos(ang).astype(np.float32)
        sin[d, CT:] = (-np.sin(ang) if first else np.sin(ang)).astype(np.float32)
        partner[d] = d + hh if first else d - hh
    return cos, sin, partner


def _prep_shared(inp):
    f = np.float32
    cos_g, sin_g, part_g = _rope_tables(64)
    cos_m, sin_m, part_m = _rope_tables(32)
    sh = {}
    sh["cosg"] = np.ascontiguousarray(np.concatenate([cos_g, cos_g], 0))
    sh["sing"] = np.ascontiguousarray(np.concatenate([sin_g, sin_g], 0))
    sh["cosm"] = cos_m
    sh["sinm"] = sin_m
    w_in = inp["w_in"]
    o_av, o_ag, o_cq, o_ckv, o_kpe, o_q, o_k, o_v, o_rq, o_rk, o_rv, o_gf, o_gb = (
        0, 256, 512, 768, 896, 928, 1184, 1312, 1440, 1568, 1696, 1952, 2208)
    cols = []
    cols += list(range(o_av, o_av + 256)) + list(range(o_ag, o_ag + 256)) + list(range(o_cq, o_cq + 256)) + list(range(o_ckv, o_ckv + 128))
    qhead_order = [0, 2, 1, 3]
    qa = [o_q + h * 64 + d for h in qhead_order for d in range(64)]
    qb = [o_q + h * 64 + int(part_g[d]) for h in qhead_order for d in range(64)]
    ka = [o_k + h * 64 + d for h in range(2) for d in range(64)]
    kb = [o_k + h * 64 + int(part_g[d]) for h in range(2) for d in range(64)]
    cols += qa + qb + ka + kb
    cols += list(range(o_v, o_v + 128)) + list(range(o_rq, o_rq + 128)) + list(range(o_rk, o_rk + 128))
    cols += list(range(o_rv, o_rv + 256)) + list(range(o_gf, o_gf + 256)) + list(range(o_gb, o_gb + 256))
    cols += [o_kpe + d for d in range(32)] + [o_kpe + int(part_m[d]) for d in range(32)]
    assert len(cols) == NU
    sh["w_in"] = np.ascontiguousarray(w_in[:, :, np.array(cols)])
    uq = inp["mla_uq"]
    ub_cols = []
    for h in range(4):
        ub_cols += [h * 96 + c for c in range(64)] + [h * 96 + 64 + int(part_m[d]) for d in range(32)]
    sh["uqA"] = np.ascontiguousarray(uq)
    sh["uqB"] = np.ascontiguousarray(uq[:, :, np.array(ub_cols)])
    sh["ukv"] = np.ascontiguousarray(inp["mla_ukv"])
    sh["w_mod"] = np.ascontiguousarray(inp["w_mod"])
    sh["b_modT"] = np.ascontiguousarray(inp["b_mod"].reshape(DEPTH, 48, 128).transpose(2, 0, 1))
    sh["w_out"] = np.ascontiguousarray(inp["w_out"])
    sh["conv_dwT"] = np.ascontiguousarray(inp["conv_dw"].reshape(DEPTH, 31, 2, 128).transpose(0, 3, 2, 1))
    sh["conv_pw"] = np.ascontiguousarray(inp["conv_pw"])
    sh["w1"] = np.ascontiguousarray(inp["mlp_w1"])
    sh["w2"] = np.ascontiguousarray(inp["mlp_w2"])
    vecs = np.zeros((128, DEPTH, NV), f)

    def pc(v):
        return v.reshape(DEPTH, 2, 128).transpose(2, 0, 1)

    vecs[:, :, 0:2] = pc(inp["conv_b"])
    vecs[:, :, 2:4] = pc(inp["conv_ln_g"])
    vecs[:, :, 4:6] = pc(inp["conv_ln_b"])
    vecs[:, :, 6:8] = pc(inp["mla_q_g"])
    vecs[:, :, 8] = inp["mla_kv_g"].T
    gq = inp["gqa_q_g"]
    gk = inp["gqa_k_g"]
    vecs[:, :, 9] = np.concatenate([gq, gq], 1).T
    vecs[:, :, 10] = np.concatenate([gq[:, part_g], gq[:, part_g]], 1).T
    vecs[:, :, 11] = np.concatenate([gk, gk], 1).T
    vecs[:, :, 12] = np.concatenate([gk[:, part_g], gk[:, part_g]], 1).T
    rn = inp["ret_norm_g"]
    rd = inp["ret_decay"]
    for d in range(2):
        for h in range(4):
            vecs[:, :, 13 + d * 4 + h] = np.concatenate([rn[:, d, h, :], rn[:, d, h, :]], 1).T
            vecs[:, :, 21 + d * 4 + h] = np.broadcast_to(rd[:, d, h][None, :], (128, DEPTH))
    sh["vecs"] = vecs
    sh["fing"] = np.ascontiguousarray(inp["final_g"].reshape(8, 128).T)
    sh["ident"] = np.eye(128, dtype=f)
    blk = np.zeros((128, 128), f)
    blk[:64, :64] = 1
    blk[64:, 64:] = 1
    sh["blk"] = blk
    p = np.arange(128)[:, None].astype(f)
    i = np.arange(128)[None, :].astype(f)
    cst = np.zeros((128, NCONST), f)
    cst[:, 0:128] = np.maximum(i - p, 0)
    cst[:, 128:256] = np.maximum(p - i, 0)
    cst[:, 256:384] = (i >= p)
    cst[:, 384:512] = (i <= p)
    cst[:, 512:640] = i + 1
    cst[:, 640:768] = 128 - i
    cst[:, 768] = 127 - p[:, 0]
    cst[:, 769] = p[:, 0]
    sh["consts"] = cst
    return sh


def _prep_core(inp, b):
    xin = np.ascontiguousarray(np.concatenate([inp["ctx"][b], inp["x"][b]], 0).T)
    cc = np.stack([inp["c"][b].reshape(8, 128).T, inp["c_ctx"].reshape(8, 128).T], -1)
    return {"xin": xin, "cc": np.ascontiguousarray(cc.astype(np.float32))}


_NC_CACHE = {}


def kernel(**inputs):
    inp = {k: np.asarray(v, dtype=np.float32) for k, v in inputs.items()}
    sh = _prep_shared(inp)
    if "nc" not in _NC_CACHE:
        _NC_CACHE["nc"] = build()
    nc = _NC_CACHE["nc"]
    in_maps = []
    for b in range(8):
        m = dict(sh)
        m.update(_prep_core(inp, b))
        in_maps.append(m)
    res = run_bass_kernel_spmd(nc, in_maps, core_ids=list(range(8)))
    out = np.stack([np.ascontiguousarray(res.results[b]["out"].T) for b in range(8)], 0)
    return out.astype(np.float32)
```
